# Optimizing a Trainium2 kernel written in Bass

```python
import jax, jax.numpy as jnp
from jax import lax
import numpy as np

D_MODEL = 2048
BATCH = 8
SEQ = 4096
DEPTH = 4

N_MIXERS = 4
RMS_EPS = 1e-6
HGRN_EXPAND = 128
HGRN_HEADS = D_MODEL // HGRN_EXPAND
HGRN_DK = HGRN_EXPAND
HGRN_DV = D_MODEL // HGRN_HEADS
HGRN_WIDTH = HGRN_HEADS * HGRN_DK
HGRN_CHUNK = 64
SWA_HEAD_DIM = 64
SWA_Q_HEADS = D_MODEL // SWA_HEAD_DIM
SWA_KV_HEADS = SWA_Q_HEADS // 8
SWA_WINDOW = 128
SCONV_WIDTH = 3
FOX_HEAD_DIM = 64
FOX_HEADS = D_MODEL // FOX_HEAD_DIM
FOX_BLOCK = 128
ROPE_THETA = 500000.0
ROT_DIM = SWA_HEAD_DIM // 4
D_FF = 5632
FFN_CONV_WIDTH = 3

kernel_name = "hybrid_interleaved_hgrn2_swa_sconv_fox"


def rmsnorm(x, g):
    xf = x.astype(jnp.float32)
    y = xf * lax.rsqrt(jnp.mean(xf * xf, axis=-1, keepdims=True) + RMS_EPS)
    return (y * g.astype(jnp.float32)).astype(x.dtype)


def causal_dwconv(x, w):
    K, C = w.shape
    return lax.conv_general_dilated(
        x, w[:, None, :].astype(x.dtype), window_strides=(1,), padding=[(K - 1, 0)],
        dimension_numbers=("NWC", "WIO", "NWC"), feature_group_count=C)


def partial_rope(x, positions):
    half = ROT_DIM // 2
    inv_freq = ROPE_THETA ** (-jnp.arange(half, dtype=jnp.float32) / half)
    ang = positions.astype(jnp.float32)[:, None] * inv_freq[None, :]
    cos = jnp.cos(ang)[None, :, None, :]
    sin = jnp.sin(ang)[None, :, None, :]
    xf = x.astype(jnp.float32)
    x1, x2 = xf[..., :half], xf[..., half:ROT_DIM]
    out = jnp.concatenate([x1 * cos - x2 * sin, x2 * cos + x1 * sin, xf[..., ROT_DIM:]], axis=-1)
    return out.astype(x.dtype)


def hgrn2_mixer(h, w_in, w_out, norm_g, lb):
    Bsz, T, _ = h.shape
    C = HGRN_CHUNK
    nC = T // C
    q, f, i, g = jnp.split(h @ w_in, 4, axis=-1)
    q = jax.nn.silu(q.astype(jnp.float32))
    f = lb + (1.0 - lb) * jax.nn.sigmoid(f.astype(jnp.float32))
    log_f = jnp.log(f)
    k = 1.0 - f
    v = i.astype(jnp.float32)

    def to_chunks(a, d):
        return a.reshape(Bsz, nC, C, HGRN_HEADS, d).transpose(1, 0, 3, 2, 4)

    qc, kc, vc = to_chunks(q, HGRN_DK), to_chunks(k, HGRN_DK), to_chunks(v, HGRN_DV)
    bc = jnp.cumsum(to_chunks(log_f, HGRN_DK), axis=3)
    causal = jnp.tril(jnp.ones((C, C), dtype=bool))

    def chunk_step(S, inp):
        qb, kb, vb, bb = inp
        inter = jnp.einsum("bhtk,bhkv->bhtv", qb * jnp.exp(bb), S)
        rel = jnp.where(causal[:, :, None], bb[:, :, :, None, :] - bb[:, :, None, :, :], -jnp.inf)
        A = jnp.einsum("bhtk,bhsk,bhtsk->bhts", qb, kb, jnp.exp(rel))
        intra = jnp.einsum("bhts,bhsv->bhtv", A, vb)
        b_last = bb[:, :, -1:, :]
        S_new = jnp.exp(b_last[:, :, 0, :])[..., None] * S + jnp.einsum(
            "bhsk,bhsv->bhkv", kb * jnp.exp(b_last - bb), vb)
        return S_new, inter + intra

    S0 = jnp.zeros((Bsz, HGRN_HEADS, HGRN_DK, HGRN_DV), jnp.float32)
    _, o = lax.scan(chunk_step, S0, (qc, kc, vc, bc))
    o = o.transpose(1, 0, 3, 2, 4).reshape(Bsz, T, HGRN_HEADS, HGRN_DV)
    o = rmsnorm(o, norm_g).reshape(Bsz, T, HGRN_HEADS * HGRN_DV)
    o = o * jax.nn.silu(g.astype(jnp.float32))
    return o.astype(h.dtype) @ w_out


def swa_sink_mixer(h, positions, w_in, w_out, sinks):
    Bsz, T, _ = h.shape
    W, d = SWA_WINDOW, SWA_HEAD_DIM
    KV, G = SWA_KV_HEADS, SWA_Q_HEADS // SWA_KV_HEADS
    nblk = T // W
    q, k, v = jnp.split(h @ w_in, [SWA_Q_HEADS * d, SWA_Q_HEADS * d + KV * d], axis=-1)
    q = partial_rope(q.reshape(Bsz, T, SWA_Q_HEADS, d), positions)
    k = partial_rope(k.reshape(Bsz, T, KV, d), positions)
    v = v.reshape(Bsz, T, KV, d)
    qb = q.reshape(Bsz, nblk, W, KV, G, d)

    def band(a):
        cur = a.reshape(Bsz, nblk, W, KV, d)
        prev = jnp.concatenate([jnp.zeros_like(cur[:, :1]), cur[:, :-1]], axis=1)
        return jnp.concatenate([prev, cur], axis=2)

    kb, vb = band(k), band(v)
    s = jnp.einsum("bnqhgd,bnkhd->bnhgqk", qb, kb).astype(jnp.float32) * (d ** -0.5)
    qi = jnp.arange(W)[:, None]
    kj = jnp.arange(2 * W)[None, :]
    diff = qi + W - kj
    blk = jnp.arange(nblk)[:, None, None]
    allowed = (diff >= 0) & (diff < W) & (blk * W + kj - W >= 0)
    s = jnp.where(allowed[None, :, None, None], s, -jnp.inf)
    sink = sinks.astype(jnp.float32).reshape(KV, G)[None, None, :, :, None, None]
    m = jnp.maximum(jnp.max(s, axis=-1, keepdims=True), sink)
    e = jnp.exp(s - m)
    p = e / (jnp.sum(e, axis=-1, keepdims=True) + jnp.exp(sink - m))
    o = jnp.einsum("bnhgqk,bnkhd->bnqhgd", p.astype(h.dtype), vb)
    return o.reshape(Bsz, T, SWA_Q_HEADS * d) @ w_out


def short_conv_mixer(h, w_in, conv_w, w_out):
    b_gate, c_gate, xv = jnp.split(h @ w_in, 3, axis=-1)
    return (b_gate * causal_dwconv(c_gate * xv, conv_w)) @ w_out


def fox_mixer(h, w_in, b_f, w_out):
    Bsz, T, _ = h.shape
    H, d, W = FOX_HEADS, FOX_HEAD_DIM, FOX_BLOCK
    width = H * d
    nblk = T // W
    q, k, v, f_logit, g = jnp.split(h @ w_in, [width, 2 * width, 3 * width, 3 * width + H], axis=-1)
    q = q.reshape(Bsz, T, H, d)
    k = k.reshape(Bsz, T, H, d)
    v = v.reshape(Bsz, T, H, d)
    log_f = jax.nn.log_sigmoid(f_logit.astype(jnp.float32) + b_f.astype(jnp.float32))
    c = jnp.cumsum(log_f, axis=1).transpose(0, 2, 1)
    key_pos = jnp.arange(T)

    def q_block(n):
        start = n * W
        qs = lax.dynamic_slice_in_dim(q, start, W, axis=1)
        cq = lax.dynamic_slice_in_dim(c, start, W, axis=2)
        s = jnp.einsum("bqhd,bkhd->bhqk", qs, k).astype(jnp.float32) * (d ** -0.5)
        s = s + cq[..., None] - c[:, :, None, :]
        q_pos = start + jnp.arange(W)
        s = jnp.where((key_pos[None, :] <= q_pos[:, None])[None, None], s, -jnp.inf)
        p = jax.nn.softmax(s, axis=-1)
        return jnp.einsum("bhqk,bkhd->bqhd", p.astype(v.dtype), v)

    o = lax.map(q_block, jnp.arange(nblk))
    o = o.transpose(1, 0, 2, 3, 4).reshape(Bsz, T, width)
    o = o * jax.nn.sigmoid(g.astype(jnp.float32)).astype(o.dtype)
    return o @ w_out


def conv_glu_ffn(h, w_up, conv_w, conv_b, w_down):
    u = causal_dwconv(h @ w_up, conv_w) + conv_b.astype(h.dtype)
    gate, up = jnp.split(u, 2, axis=-1)
    return (jax.nn.silu(gate) * up) @ w_down


def setup_inputs(seed: int = 0) -> dict:
    key = jax.random.key(seed)
    ks = iter(jax.random.split(key, 32))

    def nrm(shape, scale):
        return scale * jax.random.normal(next(ks), shape, jnp.float32)

    n_of = [len(range(m, DEPTH, N_MIXERS)) for m in range(N_MIXERS)]
    nA, nB, nC, nD = n_of
    D = D_MODEL
    sd = D ** -0.5
    return {
        "x": nrm((BATCH, SEQ, D), 1.0),
        "positions": jnp.arange(SEQ, dtype=jnp.int32),
        "mix_pre_g": 1.0 + nrm((DEPTH, D), 0.05),
        "mix_post_g": 1.0 + nrm((DEPTH, D), 0.05),
        "ffn_pre_g": 1.0 + nrm((DEPTH, D), 0.05),
        "ffn_post_g": 1.0 + nrm((DEPTH, D), 0.05),
        "hgrn_w_in": nrm((nA, D, 3 * HGRN_WIDTH + HGRN_HEADS * HGRN_DV), sd),
        "hgrn_w_out": nrm((nA, HGRN_HEADS * HGRN_DV, D), (HGRN_HEADS * HGRN_DV) ** -0.5),
        "hgrn_norm_g": 1.0 + nrm((nA, HGRN_DV), 0.05),
        "hgrn_lb_param": nrm((DEPTH + 1, HGRN_WIDTH), 0.5),
        "swa_w_in": nrm((nB, D, (SWA_Q_HEADS + 2 * SWA_KV_HEADS) * SWA_HEAD_DIM), sd),
        "swa_w_out": nrm((nB, SWA_Q_HEADS * SWA_HEAD_DIM, D), (SWA_Q_HEADS * SWA_HEAD_DIM) ** -0.5),
        "swa_sinks": nrm((nB, SWA_Q_HEADS), 0.5),
        "sc_w_in": nrm((nC, D, 3 * D), sd),
        "sc_conv_w": nrm((nC, SCONV_WIDTH, D), SCONV_WIDTH ** -0.5),
        "sc_w_out": nrm((nC, D, D), sd),
        "fox_w_in": nrm((nD, D, 4 * FOX_HEADS * FOX_HEAD_DIM + FOX_HEADS), sd),
        "fox_b_f": 3.0 + nrm((nD, FOX_HEADS), 0.5),
        "fox_w_out": nrm((nD, FOX_HEADS * FOX_HEAD_DIM, D), (FOX_HEADS * FOX_HEAD_DIM) ** -0.5),
        "ffn_w_up": nrm((DEPTH, D, 2 * D_FF), sd),
        "ffn_conv_w": nrm((DEPTH, FFN_CONV_WIDTH, 2 * D_FF), FFN_CONV_WIDTH ** -0.5),
        "ffn_conv_b": nrm((DEPTH, 2 * D_FF), 0.02),
        "ffn_w_down": nrm((DEPTH, D_FF, D), D_FF ** -0.5),
    }


def reference(x, positions, mix_pre_g, mix_post_g, ffn_pre_g, ffn_post_g,
              hgrn_w_in, hgrn_w_out, hgrn_norm_g, hgrn_lb_param,
              swa_w_in, swa_w_out, swa_sinks,
              sc_w_in, sc_conv_w, sc_w_out,
              fox_w_in, fox_b_f, fox_w_out,
              ffn_w_up, ffn_conv_w, ffn_conv_b, ffn_w_down):
    lb_table = jnp.cumsum(jax.nn.softmax(hgrn_lb_param.astype(jnp.float32), axis=0), axis=0)
    for i in range(DEPTH):
        m, j = i % N_MIXERS, i // N_MIXERS
        hn = rmsnorm(x, mix_pre_g[i])
        if m == 0:
            y = hgrn2_mixer(hn, hgrn_w_in[j], hgrn_w_out[j], hgrn_norm_g[j], lb_table[i])
        elif m == 1:
            y = swa_sink_mixer(hn, positions, swa_w_in[j], swa_w_out[j], swa_sinks[j])
        elif m == 2:
            y = short_conv_mixer(hn, sc_w_in[j], sc_conv_w[j], sc_w_out[j])
        else:
            y = fox_mixer(hn, fox_w_in[j], fox_b_f[j], fox_w_out[j])
        x = x + rmsnorm(y.astype(x.dtype), mix_post_g[i])
        hn = rmsnorm(x, ffn_pre_g[i])
        y = conv_glu_ffn(hn, ffn_w_up[i], ffn_conv_w[i], ffn_conv_b[i], ffn_w_down[i])
        x = x + rmsnorm(y.astype(x.dtype), ffn_post_g[i])
    return x
```

```python
import numpy as np
import ml_dtypes
import concourse.bass as bass
import concourse.mybir as mybir
from concourse.bass_utils import run_bass_kernel_spmd
from contextlib import ExitStack

F32 = mybir.dt.float32
BF16 = mybir.dt.bfloat16
AF = mybir.ActivationFunctionType
ALU = mybir.AluOpType
AX = mybir.AxisListType

D = 2048
DFF = 5632
NCH = D // 128
FCH = DFF // 128
EPS = 1e-6
NT = 512
NCORES = 8
SEQ = 4096
DEPTH = 4
WIN_COLS = {0: 8192, 1: 2816, 2: 6144, 3: 8224}


class _Op:
    __slots__ = ("eng", "fn", "deps", "dma", "sig", "idx")

    def __init__(self, eng, fn, deps, dma):
        self.eng = eng
        self.fn = fn
        self.deps = deps
        self.dma = dma
        self.sig = False
        self.idx = 0


class Sched:
    ENGS = ("pe", "act", "dve", "pool", "sp")

    def __init__(self, nc, stack):
        self.nc = nc
        self.stack = stack
        self.sems = {}
        self.cnt = {}
        self.bar = stack.enter_context(nc.semaphore("bar"))
        self.nbar = 0
        self.begin()

    def begin(self):
        self.ops = []
        self.W = {}
        self.R = {}

    def _sem(self, key):
        if key not in self.sems:
            self.sems[key] = self.stack.enter_context(self.nc.semaphore("s%d" % len(self.sems)))
            self.cnt[key] = 0
        return self.sems[key]

    def op(self, eng, fn, reads=(), writes=(), dma=None):
        idx = len(self.ops)
        deps = set()
        bank_reads = [t for t in reads if isinstance(t, tuple) and t[0] == "bank"]
        if bank_reads:
            reads = [t for t in reads if t not in bank_reads]
            writes = list(writes) + [t for t in bank_reads if t not in writes]
        for t in reads:
            w = self.W.get(t)
            if w:
                deps |= w
        for t in writes:
            w = self.W.get(t)
            if w:
                deps |= w
            r = self.R.get(t)
            if r:
                for k, v in r.items():
                    if k is None:
                        deps.update(v)
                    else:
                        deps.add(v)
        for t in writes:
            r = self.R.get(t)
            if r:
                self.W[t] = {idx}
                self.R[t] = {}
            else:
                self.W.setdefault(t, set()).add(idx)
        for t in reads:
            r = self.R.setdefault(t, {})
            if dma is not None:
                r.setdefault(None, []).append(idx)
            else:
                r[eng] = idx
        deps.discard(idx)
        self.ops.append(_Op(eng, fn, deps, dma))
        return idx

    def emit(self):
        nc = self.nc
        ops = self.ops
        per_eng = {e: [] for e in self.ENGS}
        for i, o in enumerate(ops):
            per_eng[o.eng].append(i)
            for d in o.deps:
                ops[d].sig = True
        for e in self.ENGS:
            lst = [i for i in per_eng[e] if ops[i].dma is None]
            if lst:
                ops[lst[-1]].sig = True
        used = []
        for o in ops:
            key = ("dma", o.dma) if o.dma is not None else ("eng", o.eng)
            if o.dma is not None or o.sig:
                self._sem(key)
                if key not in used:
                    used.append(key)
                self.cnt[key] += 1
                o.idx = self.cnt[key]
        sems = self.sems
        final = {k: self.cnt[k] for k in used}
        self.nbar += 1
        nbar = self.nbar
        bar = self.bar

        def run_engine(e, engobj):
            seen = {}
            for i in per_eng[e]:
                o = ops[i]
                need = {}
                for d in o.deps:
                    p = ops[d]
                    if p.dma is not None:
                        key = ("dma", p.dma)
                        val = 16 * p.idx
                    else:
                        key = ("eng", p.eng)
                        val = p.idx
                    if need.get(key, 0) < val:
                        need[key] = val
                for key, val in need.items():
                    if seen.get(key, 0) < val:
                        engobj.wait_ge(sems[key], val)
                        seen[key] = val
                ins = o.fn(engobj)
                if o.dma is not None:
                    ins.then_inc(sems[("dma", o.dma)], 16)
                elif o.sig:
                    ins.then_inc(sems[("eng", o.eng)], 1)
            if e == "sp":
                for key, v in final.items():
                    engobj.wait_ge(sems[key], 16 * v if key[0] == "dma" else v)
                engobj.sem_inc(bar, 1)
            else:
                engobj.wait_ge(bar, nbar)

        with nc.Block() as block:
            @block.tensor
            def _(eng):
                run_engine("pe", eng)

            @block.scalar
            def _(eng):
                run_engine("act", eng)

            @block.vector
            def _(eng):
                run_engine("dve", eng)

            @block.gpsimd
            def _(eng):
                run_engine("pool", eng)

            @block.sync
            def _(eng):
                run_engine("sp", eng)
        self.begin()


class KB:
    def __init__(self, T, layers):
        self.T = T
        self.layers = layers
        self.nc = bass.Bass("TRN2", target_bir_lowering=False)
        self.uid = 0
        self.inputs = {}

    def din(self, name, shape, dt=F32):
        t = self.nc.dram_tensor(name, list(shape), dt, kind="ExternalInput").ap()
        self.inputs[name] = t
        return t

    def name(self, s):
        self.uid += 1
        return "%s_%d" % (s, self.uid)


def _pre_norm_tile(kb, S, A, P, src, tt, gT, ident, hnT, xs, hnb, junk, stat, epsc, subs=None):
    nsub = NT // 128
    for s in (range(nsub) if subs is None else subs):
        r0 = tt * NT + s * 128
        sl = s % 2
        S.op("sp", lambda e, sl=sl, r0=r0: e.dma_start(out=xs[sl][:], in_=src[r0:r0 + 128, :]),
             writes=[("xs", sl)], dma="xs%d" % sl)
        S.op("act", lambda e, sl=sl: e.activation(out=junk[:], in_=xs[sl][:], func=AF.Square,
                                                  accum_out=stat[:, sl:sl + 1]),
             reads=[("xs", sl)], writes=["junk", ("ss", sl)])
        S.op("act", lambda e, sl=sl: e.activation(out=stat[:, 2 + sl:3 + sl], in_=stat[:, sl:sl + 1],
                                                  func=AF.Sqrt, scale=1.0 / D, bias=epsc[:, 0:1]),
             reads=[("ss", sl), "epsc"], writes=[("ms", sl)])
        S.op("dve", lambda e, sl=sl: e.reciprocal(out=stat[:, 4 + sl:5 + sl], in_=stat[:, 2 + sl:3 + sl]),
             reads=[("ms", sl)], writes=[("rstd", sl)])
        S.op("dve", lambda e, sl=sl: e.tensor_scalar(out=hnb[sl][:], in0=xs[sl][:], scalar1=stat[:, 4 + sl:5 + sl],
                                                     scalar2=None, op0=ALU.mult),
             reads=[("xs", sl), ("rstd", sl)], writes=[("hnb", sl)])
        for g in range(2):
            pt = P["tr"][g]

            def tr(e, g=g, sl=sl, pt=pt):
                for j in range(8):
                    c = g * 8 + j
                    ins = e.transpose(pt[:, j * 128:(j + 1) * 128], hnb[sl][:, c * 128:(c + 1) * 128], ident[:])
                return ins
            S.op("pe", tr, reads=[("hnb", sl), "ident"], writes=[("bank", P["trbank"][g])])
            S.op("dve", lambda e, g=g, s=s, pt=pt: e.tensor_tensor(
                out=hnT[:, g * 8:(g + 1) * 8, s * 128:(s + 1) * 128],
                in0=pt.rearrange("p (c t) -> p c t", c=8),
                in1=gT[:, g * 8:(g + 1) * 8].unsqueeze(2).broadcast_to([128, 8, 128]), op=ALU.mult),
                reads=[("bank", P["trbank"][g]), "gT"], writes=[("hnT", s, g)])


def _post_norm_residual(kb, S, ybuf, s, src, dst, r0, grow, xs, junk, stat, ykey, epsc):
    sl = s % 2
    S.op("sp", lambda e: e.dma_start(out=xs[sl][:], in_=src[r0:r0 + 128, :]),
         writes=[("xs", sl)], dma="xs%d" % sl)
    S.op("act", lambda e: e.activation(out=junk[:], in_=ybuf[:, s, :], func=AF.Square,
                                       accum_out=stat[:, 6 + sl:7 + sl]),
         reads=[ykey], writes=["junk", ("pss", sl)])
    S.op("act", lambda e: e.activation(out=stat[:, 8 + sl:9 + sl], in_=stat[:, 6 + sl:7 + sl],
                                       func=AF.Sqrt, scale=1.0 / D, bias=epsc[:, 0:1]),
         reads=[("pss", sl), "epsc"], writes=[("pms", sl)])
    S.op("dve", lambda e: e.reciprocal(out=stat[:, 10 + sl:11 + sl], in_=stat[:, 8 + sl:9 + sl]),
         reads=[("pms", sl)], writes=[("prstd", sl)])
    S.op("dve", lambda e: e.scalar_tensor_tensor(out=ybuf[:, s, :], in0=ybuf[:, s, :],
                                                 scalar=stat[:, 10 + sl:11 + sl], in1=grow[:],
                                                 op0=ALU.mult, op1=ALU.mult),
         reads=[ykey, ("prstd", sl), "grow"], writes=[ykey])
    S.op("dve", lambda e: e.tensor_tensor(out=xs[sl][:], in0=xs[sl][:], in1=ybuf[:, s, :], op=ALU.add),
         reads=[ykey, ("xs", sl)], writes=[("xs", sl)])
    S.op("sp", lambda e: e.dma_start(out=dst[r0:r0 + 128, :], in_=xs[sl][:]),
         reads=[("xs", sl)], writes=[("dram", id(dst.tensor), r0)], dma="xs%d" % sl)


def _out_proj_tile(kb, S, P, lhs_fn, nk, wsrc, wslots, wname, kpiece, ybuf, lhs_keys, after_dc=None):
    nsub = NT // 128
    npiece = nk // kpiece
    cnt = kb.wcnt
    for dc in range(4):
        for pc in range(npiece):
            slot = cnt[wname] % 2
            cnt[wname] += 1
            wt = wslots[slot]
            S.op("pool", lambda e, wt=wt, pc=pc, dc=dc: e.dma_start(
                out=wt[:, 0:kpiece, :], in_=wsrc[:, pc * kpiece:(pc + 1) * kpiece, dc * 512:(dc + 1) * 512],
                max_dma_last_dim=8192),
                writes=[(wname, slot)], dma="%s%d" % (wname, slot))
            for s in range(nsub):
                def mm(e, wt=wt, pc=pc, s=s):
                    for k in range(kpiece):
                        kk = pc * kpiece + k
                        ins = e.matmul(P["dn"][s][:], lhs_fn(kk, s), wt[:, k, :],
                                       start=(kk == 0), stop=(kk == nk - 1))
                    return ins
                S.op("pe", mm, reads=[(wname, slot)] + lhs_keys(pc, s), writes=[("bank", P["dnbank"][s])])
        for s in range(nsub):
            S.op("act", lambda e, s=s, dc=dc: e.copy(out=ybuf[:, s, dc * 512:(dc + 1) * 512], in_=P["dn"][s][:]),
                 reads=[("bank", P["dnbank"][s])], writes=[("ybuf", s)])
        if after_dc is not None:
            after_dc(dc)


def emit_weight_convert(kb, S, l, part, nparts):
    if l not in kb.wb:
        return
    wupb, wdnb = kb.wb[l]
    wup = kb.inputs["wup%d" % l]
    wdn = kb.inputs["wdn%d" % l]
    pieces = [(wup, wupb, r) for r in range(D // 128)] + [(wdn, wdnb, r) for r in range(DFF // 128)]
    for idx in range(part, len(pieces), nparts):
        src_, dst_, r = pieces[idx]
        S.op("pool", lambda e, src_=src_, dst_=dst_, r=r: e.dma_start(
            out=dst_[r * 128:(r + 1) * 128, :], in_=src_[r * 128:(r + 1) * 128, :], max_dma_last_dim=8192),
            dma="cv%d" % (idx % 4))


def ffn_phase(kb, S, l, src, dst):
    nc = kb.nc
    T = kb.T
    ntile = T // NT
    nsub = NT // 128
    GP = 2
    if kb.layers[l] is None:
        for part in range(4):
            emit_weight_convert(kb, S, l, part, 4)
        S.emit()
    wup = kb.wb[l][0].rearrange("(c p) f -> p c f", p=128)
    wdn = kb.wb[l][1].rearrange("(c p) f -> p c f", p=128)
    pk_d = kb.inputs["pk%d" % l]
    rows_d = kb.inputs["rows%d" % l]
    with ExitStack() as st:
        def A(nm, shape, dt):
            return st.enter_context(nc.sbuf_tensor(kb.name(nm), shape, dt))
        xs = [A("xs", [128, D], F32) for _ in range(2)]
        hnb = [A("hnb", [128, D], BF16) for _ in range(2)]
        junk = A("junk", [128, D], BF16)
        stat = A("stat", [128, 16], F32)
        hnT = A("hnT", [128, NCH, NT], BF16)
        gbuf = A("g", [128, FCH, NT], BF16)
        wu = [A("wu", [128, NCH, 2, GP * 128], BF16) for _ in range(2)]
        wd = [A("wd", [128, 11, 512], BF16) for _ in range(2)]
        ybuf = A("ybuf", [128, nsub, D], F32)
        grow = A("grow", [128, D], F32)
        pk = A("pk", [128, 512], F32)
        ident = A("ident", [128, 128], BF16)
        ub = [A("ub", [128, NT + 2], F32) for _ in range(2)]
        acc = [A("acc", [128, NT], F32) for _ in range(2)]
        sg = A("sg", [128, NT], F32)
        halo = A("halo", [128, 2 * FCH, 2], F32)
        pup = st.enter_context(nc.psum_tensor(kb.name("pup"), [128, 4, 512], F32))
        pdn = st.enter_context(nc.psum_tensor(kb.name("pdn"), [128, 4, 512], F32))
        P = {"tr": [pup[:, 0, :].bitcast(BF16), pup[:, 1, :].bitcast(BF16)], "trbank": [0, 1],
             "dn": [pdn[:, s, :] for s in range(4)], "dnbank": [4, 5, 6, 7]}
        gT = pk[:, 16:32]
        cw = pk[:, 32:32 + 2 * FCH * 3].rearrange("p (c k) -> p c k", k=3)
        cb = pk[:, 296:296 + 2 * FCH]

        S.op("sp", lambda e: e.dma_start(out=pk[:], in_=pk_d), writes=["gT", "pk"], dma="pk")
        S.op("sp", lambda e: e.dma_start(out=grow[:], in_=rows_d[1].partition_broadcast(128)),
             writes=["grow"], dma="grow")
        S.op("sp", lambda e: e.dma_start(out=ident[:], in_=kb.inputs["ident"]), writes=["ident"], dma="ident")
        S.op("dve", lambda e: e.memset(halo[:], 0.0), writes=["halo"])
        epsc = A("epsc", [128, 1], F32)
        S.op("dve", lambda e: e.memset(epsc[:], EPS), writes=["epsc"])
        kb.wcnt = {"wu": 0, "wd": 0}
        _pre_norm_tile(kb, S, A, P, src, 0, gT, ident, hnT, xs, hnb, junk, stat, epsc)
        deferred = []
        for tt in range(ntile):
            hn_keys = [("hnT", s, g) for s in range(nsub) for g in range(2)]
            for grp in range(FCH // GP):
                slot = kb.wcnt["wu"] % 2
                kb.wcnt["wu"] += 1
                wt = wu[slot]
                for half in range(2):
                    c0 = half * DFF + grp * GP * 128
                    S.op("pool", lambda e, wt=wt, half=half, c0=c0: e.dma_start(
                        out=wt[:, :, half, :], in_=wup[:, :, c0:c0 + GP * 128]),
                        writes=[("wu", slot, half)], dma="wu%d_%d" % (slot, half))
                for j in range(GP):
                    fc = grp * GP + j
                    pb = (fc % 2) * 2
                    for half in range(2):
                        def mm(e, wt=wt, half=half, j=j, pb=pb):
                            for k in range(NCH):
                                ins = e.matmul(pup[:, pb + half, :], wt[:, k, half, j * 128:(j + 1) * 128],
                                               hnT[:, k, :], start=(k == 0), stop=(k == NCH - 1))
                            return ins
                        S.op("pe", mm, reads=[("wu", slot, half)] + hn_keys, writes=[("bank", pb + half)])
                    for half in range(2):
                        ch = half * FCH + fc
                        u = ub[half]
                        a = acc[half]
                        pt = pup[:, pb + half, :]
                        S.op("act", lambda e, u=u, pt=pt: e.copy(out=u[:, 2:NT + 2], in_=pt),
                             reads=[("bank", pb + half)], writes=[("ub", half)])
                        S.op("dve", lambda e, u=u, ch=ch: e.tensor_copy(out=u[:, 0:2], in_=halo[:, ch, :]),
                             reads=["halo", ("halo", ch)], writes=[("ubh", half)])
                        S.op("act", lambda e, a=a, pt=pt, ch=ch: e.activation(
                            out=a[:], in_=pt, func=AF.Identity, scale=cw[:, ch, 2:3], bias=cb[:, ch:ch + 1]),
                            reads=[("bank", pb + half), "pk"], writes=[("acc", half)])
                        S.op("dve", lambda e, u=u, a=a, ch=ch: e.scalar_tensor_tensor(
                            out=a[:], in0=u[:, 1:NT + 1], scalar=cw[:, ch, 1:2], in1=a[:],
                            op0=ALU.mult, op1=ALU.add),
                            reads=[("ub", half), ("ubh", half), ("acc", half), "pk"], writes=[("acc", half)])
                        S.op("dve", lambda e, u=u, a=a, ch=ch: e.scalar_tensor_tensor(
                            out=a[:], in0=u[:, 0:NT], scalar=cw[:, ch, 0:1], in1=a[:],
                            op0=ALU.mult, op1=ALU.add),
                            reads=[("ub", half), ("ubh", half), ("acc", half), "pk"], writes=[("acc", half)])
                        S.op("dve", lambda e, u=u, ch=ch: e.tensor_copy(out=halo[:, ch, :], in_=u[:, NT:NT + 2]),
                             reads=[("ub", half)], writes=[("halo", ch)])
                    S.op("act", lambda e: e.activation(out=sg[:], in_=acc[0][:], func=AF.Silu),
                         reads=[("acc", 0)], writes=["sg"])
                    S.op("dve", lambda e, fc=fc: e.tensor_tensor(out=gbuf[:, fc, :], in0=sg[:], in1=acc[1][:],
                                                                 op=ALU.mult),
                         reads=["sg", ("acc", 1)], writes=[("g", fc)])
            def nxt(dc, tt=tt):
                if tt + 1 < ntile and dc < 3:
                    _pre_norm_tile(kb, S, A, P, src, tt + 1, gT, ident, hnT, xs, hnb, junk, stat, epsc,
                                   subs=([0, 1], [2], [3])[dc])
            _out_proj_tile(kb, S, P, lambda kk, s: gbuf[:, kk, s * 128:(s + 1) * 128], FCH, wdn, wd, "wd", 11,
                           ybuf, lambda pc, s: [("g", pc * 11 + k) for k in range(11)], after_dc=nxt)
            for s in range(nsub):
                _post_norm_residual(kb, S, ybuf, s, src, dst, tt * NT + s * 128, grow, xs, junk, stat, ("ybuf", s), epsc)
        S.emit()


def build(T, layers, do_ffn=True):
    kb = KB(T, layers)
    nc = kb.nc
    x = kb.din("x", [T, D])
    kb.din("ident", [128, 128], BF16)
    for l, m in enumerate(layers):
        kb.din("pk%d" % l, [128, 512])
        kb.din("rows%d" % l, [3, D])
        if do_ffn:
            kb.din("wup%d" % l, [D, 2 * DFF])
            kb.din("wdn%d" % l, [DFF, D])
        if m is not None:
            kb.din("win%d" % l, [D, WIN_COLS[m]])
            kb.din("wout%d" % l, [D, D])
        if m == 1:
            kb.din("posT", [128, T // 128], mybir.dt.int32)
            kb.din("invf", [128, 8])
        if m in (0, 1, 3) and "swamask" not in kb.inputs:
            kb.din("swamask", [128, 256], BF16)
        if m == 3:
            kb.din("utri", [128, 128])
            kb.din("identf", [128, 128])
    kb.wb = {}
    if do_ffn:
        for l in range(len(layers)):
            kb.wb[l] = (nc.dram_tensor("wupb%d" % l, [D, 2 * DFF], BF16, kind="Internal").ap(),
                        nc.dram_tensor("wdnb%d" % l, [DFF, D], BF16, kind="Internal").ap())
    y = nc.dram_tensor("y", [T, D], F32, kind="ExternalOutput").ap()
    xres = nc.dram_tensor("xres", [T, D], F32, kind="Internal").ap()
    with ExitStack() as st:
        S = Sched(nc, st)
        cur = x
        nl = len(layers)
        for l, m in enumerate(layers):
            last = (l == nl - 1)
            if m is not None:
                mdst = y if (last and not do_ffn) else xres
                mixer_phase(kb, S, l, m, cur, mdst)
                cur = mdst
            if do_ffn:
                dst = y if last else xres
                ffn_phase(kb, S, l, cur, dst)
                cur = dst
    return kb


def _colsT(v):
    v = np.asarray(v)
    return np.ascontiguousarray(v.reshape(-1, 128).T)


def pack_layer(l, inp, m, j):
    pk = np.zeros((128, 512), np.float32)
    pk[:, 0:16] = _colsT(inp["mix_pre_g"][l])
    pk[:, 16:32] = _colsT(inp["ffn_pre_g"][l])
    cw = np.asarray(inp["ffn_conv_w"][l])
    cwT = np.stack([_colsT(cw[k]) for k in range(3)], axis=2)
    pk[:, 32:32 + 2 * FCH * 3] = cwT.reshape(128, -1)
    pk[:, 296:296 + 2 * FCH] = _colsT(inp["ffn_conv_b"][l])
    if m == 2:
        scw = np.asarray(inp["sc_conv_w"][j])
        pk[:, 384:384 + NCH * 3] = np.stack([_colsT(scw[k]) for k in range(3)], axis=2).reshape(128, -1)
    if m == 1:
        sk = np.asarray(inp["swa_sinks"][j])
        pk[:, 384:400] = np.concatenate([np.tile(sk[0::2][None, :], (64, 1)), np.tile(sk[1::2][None, :], (64, 1))], axis=0)
    if m == 0:
        lbp = np.asarray(inp["hgrn_lb_param"])
        pk[:, 384:464] = np.stack([_colsT(lbp[i]) for i in range(5)], axis=2).reshape(128, -1)
        pk[:, 464] = np.asarray(inp["hgrn_norm_g"][j])
    r3 = np.zeros((D,), np.float32)
    if m == 3:
        r3[0:32] = np.asarray(inp["fox_b_f"][j])
    rows = np.stack([np.asarray(inp["mix_post_g"][l]), np.asarray(inp["ffn_post_g"][l]), r3]).astype(np.float32)
    return pk, rows


class _Common:
    def __init__(self, kb, S, st, l, wo_k):
        nc = kb.nc
        self.st = st

        def A(nm, shape, dt):
            return st.enter_context(nc.sbuf_tensor(kb.name(nm), shape, dt))
        self.A = A
        nsub = NT // 128
        self.xs = [A("xs", [128, D], F32) for _ in range(2)]
        self.hnb = [A("hnb", [128, D], BF16) for _ in range(2)]
        self.junk = A("junk", [128, D], BF16)
        self.stat = A("stat", [128, 16], F32)
        self.hnT = A("hnT", [128, NCH, NT], BF16)
        self.ybuf = A("ybuf", [128, nsub, D], F32)
        self.grow = A("grow", [128, D], F32)
        self.pk = A("pk", [128, 512], F32)
        self.ident = A("ident", [128, 128], BF16)
        self.epsc = A("epsc", [128, 1], F32)
        self.wo = [A("wo", [128, wo_k, 512], BF16) for _ in range(2)]
        self.pA = st.enter_context(nc.psum_tensor(kb.name("pA"), [128, 4, 512], F32))
        self.pB = st.enter_context(nc.psum_tensor(kb.name("pB"), [128, 4, 512], F32))
        pB = self.pB
        pA = self.pA
        self.P = {"tr": [pA[:, 0, :].bitcast(BF16), pA[:, 1, :].bitcast(BF16)], "trbank": [0, 1],
                  "tr45": [pB[:, 0, :].bitcast(BF16), pB[:, 1, :].bitcast(BF16)],
                  "dn": [pB[:, s, :] for s in range(4)], "dnbank": [4, 5, 6, 7]}
        self.gT = self.pk[:, 0:16]
        pk, grow, ident, epsc = self.pk, self.grow, self.ident, self.epsc
        S.op("sp", lambda e: e.dma_start(out=pk[:], in_=kb.inputs["pk%d" % l]), writes=["gT", "pk"], dma="pk")
        S.op("sp", lambda e: e.dma_start(out=grow[:], in_=kb.inputs["rows%d" % l][0].partition_broadcast(128)),
             writes=["grow"], dma="grow")
        S.op("sp", lambda e: e.dma_start(out=ident[:], in_=kb.inputs["ident"]), writes=["ident"], dma="ident")
        S.op("dve", lambda e: e.memset(epsc[:], EPS), writes=["epsc"])

    def bank(self, b):
        return self.pA[:, b, :] if b < 4 else self.pB[:, b - 4, :]

    def pre_norm(self, kb, S, src, tt, subs=None):
        if tt == 0 or subs is not None or not self.pipelined:
            _pre_norm_tile(kb, S, self.A, self.P, src, tt, self.gT, self.ident, self.hnT, self.xs, self.hnb,
                           self.junk, self.stat, self.epsc, subs=subs)
        return [("hnT", s, g) for s in range(NT // 128) for g in range(2)]

    pipelined = False

    def out_proj(self, kb, S, src, dst, tt, oT, wsrc, kpiece, okeys, pipe_src=None):
        nxt = None
        if pipe_src is not None and (tt + 1) * NT < kb.T:
            self.pipelined = True

            def nxt(dc):
                if dc < 3:
                    self.pre_norm(kb, S, pipe_src, tt + 1, subs=([0, 1], [2], [3])[dc])
        _out_proj_tile(kb, S, self.P, lambda kk, s: oT[:, kk, s * 128:(s + 1) * 128], NCH, wsrc, self.wo, "wo",
                       kpiece, self.ybuf, okeys, after_dc=nxt)
        for s in range(NT // 128):
            _post_norm_residual(kb, S, self.ybuf, s, src, dst, tt * NT + s * 128, self.grow, self.xs,
                                self.junk, self.stat, ("ybuf", s), self.epsc)


def sconv_phase(kb, S, l, src, dst):
    nc = kb.nc
    ntile = kb.T // NT
    win = kb.inputs["win%d" % l].rearrange("(c p) f -> p c f", p=128)
    wout = kb.inputs["wout%d" % l].rearrange("(c p) f -> p c f", p=128)
    GP = 2
    with ExitStack() as st:
        C = _Common(kb, S, st, l, 8)
        A = C.A
        wi = [A("wi", [128, NCH, 3, GP * 128], BF16) for _ in range(2)]
        mT = A("mT", [128, NCH, NT], BF16)
        zc = A("zc", [128, NT], F32)
        ub = A("ub", [128, NT + 2], F32)
        acc = A("acc", [128, NT], F32)
        halo = A("halo", [128, NCH, 2], F32)
        cw = C.pk[:, 384:384 + NCH * 3].rearrange("p (c k) -> p c k", k=3)
        S.op("dve", lambda e: e.memset(halo[:], 0.0), writes=["halo"])
        kb.wcnt = {"wi": 0, "wo": 0}
        banksets = [(0, 1, 2), (3, 6, 7)]
        for tt in range(ntile):
            if not (l > 0 and kb.layers[l - 1] == 1):
                emit_weight_convert(kb, S, l, tt, ntile)
            hn_keys = C.pre_norm(kb, S, src, tt)
            for grp in range(NCH // GP):
                slot = kb.wcnt["wi"] % 2
                kb.wcnt["wi"] += 1
                wt = wi[slot]
                for third in range(3):
                    c0 = third * D + grp * GP * 128
                    S.op("pool", lambda e, wt=wt, third=third, c0=c0: e.dma_start(
                        out=wt[:, :, third, :], in_=win[:, :, c0:c0 + GP * 128], max_dma_last_dim=8192),
                        writes=[("wi", slot, third)], dma="wi%d_%d" % (slot, third))
                for j in range(GP):
                    ch = grp * GP + j
                    bs = banksets[ch % 2]
                    for third in range(3):
                        def mm(e, wt=wt, third=third, j=j, bs=bs):
                            for k in range(NCH):
                                ins = e.matmul(C.bank(bs[third]), wt[:, k, third, j * 128:(j + 1) * 128],
                                               C.hnT[:, k, :], start=(k == 0), stop=(k == NCH - 1))
                            return ins
                        S.op("pe", mm, reads=[("wi", slot, third)] + hn_keys, writes=[("bank", bs[third])])
                    pb_, pc_, px_ = C.bank(bs[0]), C.bank(bs[1]), C.bank(bs[2])
                    S.op("act", lambda e, pc_=pc_: e.copy(out=zc[:], in_=pc_), reads=[("bank", bs[1])], writes=["zc"])
                    S.op("dve", lambda e, px_=px_: e.tensor_tensor(out=ub[:, 2:NT + 2], in0=zc[:], in1=px_, op=ALU.mult),
                         reads=["zc", ("bank", bs[2])], writes=["ub"])
                    S.op("dve", lambda e, ch=ch: e.tensor_copy(out=ub[:, 0:2], in_=halo[:, ch, :]),
                         reads=["halo", ("halo", ch)], writes=["ubh"])
                    S.op("act", lambda e, ch=ch: e.activation(out=acc[:], in_=ub[:, 2:NT + 2], func=AF.Identity,
                                                              scale=cw[:, ch, 2:3]),
                         reads=["ub", "pk"], writes=["acc"])
                    S.op("dve", lambda e, ch=ch: e.scalar_tensor_tensor(
                        out=acc[:], in0=ub[:, 1:NT + 1], scalar=cw[:, ch, 1:2], in1=acc[:], op0=ALU.mult, op1=ALU.add),
                        reads=["ub", "ubh", "acc", "pk"], writes=["acc"])
                    S.op("dve", lambda e, ch=ch: e.scalar_tensor_tensor(
                        out=acc[:], in0=ub[:, 0:NT], scalar=cw[:, ch, 0:1], in1=acc[:], op0=ALU.mult, op1=ALU.add),
                        reads=["ub", "ubh", "acc", "pk"], writes=["acc"])
                    S.op("dve", lambda e, ch=ch: e.tensor_copy(out=halo[:, ch, :], in_=ub[:, NT:NT + 2]),
                         reads=["ub"], writes=[("halo", ch)])
                    S.op("dve", lambda e, ch=ch, pb_=pb_: e.tensor_tensor(out=mT[:, ch, :], in0=acc[:], in1=pb_, op=ALU.mult),
                         reads=["acc", ("bank", bs[0])], writes=[("oT", ch)])
            C.out_proj(kb, S, src, dst, tt, mT, wout, 8, lambda pc, s: [("oT", pc * 8 + k) for k in range(8)], pipe_src=src)
        S.emit()


import math
MAGIC = 12582912.0
PIS = 3.1415925


def swa_phase(kb, S, l, src, dst):
    nc = kb.nc
    T = kb.T
    ntile = T // NT
    nblk = T // 128
    I32 = mybir.dt.int32
    win = kb.inputs["win%d" % l].rearrange("(c p) f -> p c f", p=128)
    wout = kb.inputs["wout%d" % l].rearrange("(c p) f -> p c f", p=128)
    with ExitStack() as st:
        C = _Common(kb, S, st, l, 8)
        A = C.A
        wi = [A("wi", [128, NCH, 512], BF16) for _ in range(2)]
        qr = A("qr", [128, D], BF16)
        kr = A("kr", [128, 512], BF16)
        qT = A("qT", [128, NCH, NT], BF16)
        kT2 = A("kT2", [128, 4, NT + 128], BF16)
        VAB = A("VAB", [128, 5, 4, 2, 128], BF16)
        DAB = A("DAB", [128, 2, 128], BF16)
        mask = A("mask", [128, 256], BF16)
        oT = A("oT", [128, NCH, NT], BF16)
        PT = [A("PT", [128, 2, 256], BF16) for _ in range(2)]
        tmpA = A("tmpA", [128, 8, 16], F32)
        tmpB = A("tmpB", [128, 8, 16], F32)
        dens = A("dens", [128, NT], F32)
        esink = A("esink", [128, 16], F32)
        posi = A("posi", [128, nblk], I32)
        posf = A("posf", [128, nblk], F32)
        invf = A("invf", [128, 8], F32)
        ang = A("ang", [128, nblk, 8], F32)
        rs = A("rs", [128, nblk, 8], F32)
        rc = A("rc", [128, nblk, 8], F32)
        cs2 = A("cs2", [128, nblk, 16], F32)
        sn = A("sn", [128, nblk, 16], F32)
        S.op("sp", lambda e: e.dma_start(out=posi[:], in_=kb.inputs["posT"]), writes=["posi"], dma="posi")
        S.op("sp", lambda e: e.dma_start(out=invf[:], in_=kb.inputs["invf"]), writes=["invf"], dma="invf")
        S.op("sp", lambda e: e.dma_start(out=mask[:], in_=kb.inputs["swamask"]), writes=["mask"], dma="mask")
        S.op("dve", lambda e: e.memset(VAB[:], 0.0), writes=["VAB"])
        S.op("dve", lambda e: e.memset(DAB[:], 0.0), writes=["DAB"])
        S.op("dve", lambda e: e.memset(DAB[:, 0, 0:64], 1.0), reads=["DAB"], writes=["DAB"])
        S.op("dve", lambda e: e.memset(DAB[:, 1, 64:128], 1.0), reads=["DAB"], writes=["DAB"])
        S.op("act", lambda e: e.activation(out=esink[:], in_=C.pk[:, 384:400], func=AF.Exp), reads=["pk"], writes=["esink"])
        S.op("dve", lambda e: e.tensor_copy(out=posf[:], in_=posi[:]), reads=["posi"], writes=["posf"])
        S.op("dve", lambda e: e.tensor_tensor(out=ang[:], in0=posf[:].unsqueeze(2).broadcast_to([128, nblk, 8]),
                                              in1=invf[:].unsqueeze(1).broadcast_to([128, nblk, 8]), op=ALU.mult),
             reads=["posf", "invf"], writes=["ang"])

        def rr(dstt, shift, key):
            S.op("dve", lambda e: e.tensor_scalar(out=dstt[:], in0=ang[:], scalar1=shift, scalar2=1.0 / (2 * math.pi),
                                                  op0=ALU.add, op1=ALU.mult), reads=["ang"], writes=[key])
            S.op("dve", lambda e: e.tensor_scalar(out=dstt[:], in0=dstt[:], scalar1=MAGIC, scalar2=MAGIC,
                                                  op0=ALU.add, op1=ALU.subtract), reads=[key], writes=[key])
            S.op("dve", lambda e: e.scalar_tensor_tensor(out=dstt[:], in0=dstt[:], scalar=-2 * math.pi, in1=ang[:],
                                                         op0=ALU.mult, op1=ALU.add), reads=[key, "ang"], writes=[key])
            S.op("dve", lambda e: e.tensor_scalar(out=dstt[:], in0=dstt[:], scalar1=shift, scalar2=None, op0=ALU.add),
                 reads=[key], writes=[key])
            S.op("dve", lambda e: e.tensor_scalar(out=dstt[:], in0=dstt[:], scalar1=-PIS, scalar2=PIS,
                                                  op0=ALU.max, op1=ALU.min), reads=[key], writes=[key])
        rr(rs, 0.0, "rs")
        rr(rc, 0.5 * math.pi, "rc")
        S.op("act", lambda e: e.activation(out=sn[:, :, 8:16], in_=rs[:], func=AF.Sin), reads=["rs"], writes=["sn1"])
        S.op("act", lambda e: e.activation(out=cs2[:, :, 0:8], in_=rc[:], func=AF.Sin), reads=["rc"], writes=["cs1"])
        S.op("dve", lambda e: e.tensor_copy(out=cs2[:, :, 8:16], in_=cs2[:, :, 0:8]), reads=["cs1"], writes=["cs2"])
        S.op("dve", lambda e: e.tensor_scalar(out=sn[:, :, 0:8], in0=sn[:, :, 8:16], scalar1=-1.0, scalar2=None,
                                              op0=ALU.mult), reads=["sn1"], writes=["sn0"])
        tab_keys = ["cs1", "cs2", "sn0", "sn1"]

        def rope_evac(pv, ov, blk, pkey, okey):
            S.op("act", lambda e: e.copy(out=ov[:, :, 16:64], in_=pv[:, :, 16:64]), reads=[pkey], writes=[okey])
            S.op("dve", lambda e: e.tensor_tensor(out=tmpA[:], in0=pv[:, :, 0:16],
                                                  in1=cs2[:, blk, :].unsqueeze(1).broadcast_to([128, 8, 16]), op=ALU.mult),
                 reads=[pkey] + tab_keys, writes=["tmpA"])
            S.op("dve", lambda e: e.tensor_tensor(out=tmpB[:, :, 0:8], in0=pv[:, :, 8:16],
                                                  in1=sn[:, blk, 0:8].unsqueeze(1).broadcast_to([128, 8, 8]), op=ALU.mult),
                 reads=[pkey] + tab_keys, writes=["tmpB"])
            S.op("dve", lambda e: e.tensor_tensor(out=tmpB[:, :, 8:16], in0=pv[:, :, 0:8],
                                                  in1=sn[:, blk, 8:16].unsqueeze(1).broadcast_to([128, 8, 8]), op=ALU.mult),
                 reads=[pkey] + tab_keys, writes=["tmpB"])
            S.op("dve", lambda e: e.tensor_tensor(out=ov[:, :, 0:16], in0=tmpA[:], in1=tmpB[:], op=ALU.add),
                 reads=["tmpA", "tmpB"], writes=[okey])

        kb.wcnt = {"wi": 0, "wo": 0}
        qrs = [A("qrs", [128, 512], BF16) for _ in range(2)]
        nq_ev = 0
        for tt in range(ntile):
            emit_weight_convert(kb, S, l, tt, ntile)
            if l + 1 < len(kb.layers) and kb.layers[l + 1] == 2:
                emit_weight_convert(kb, S, l + 1, tt, ntile)
            hn_keys = C.pre_norm(kb, S, src, tt)
            first = (tt == 0)
            import os
            DBG = int(os.environ.get("KDBG", "0"))
            if DBG == 3:
                continue
            for grp in range(6):
                if DBG == 4 and grp < 5:
                    continue
                if DBG == 5 and grp != 0:
                    continue
                slot = kb.wcnt["wi"] % 2
                kb.wcnt["wi"] += 1
                wt = wi[slot]
                ncol = 256 if grp == 5 else 512
                c0 = grp * 512
                S.op("pool", lambda e, wt=wt, c0=c0, ncol=ncol: e.dma_start(
                    out=wt[:, :, 0:ncol], in_=win[:, :, c0:c0 + ncol], max_dma_last_dim=8192),
                    writes=[("wi", slot)], dma="wi%d" % slot)
                for s in range(4):
                    blk = tt * 4 + s
                    b = (grp * 4 + s) % 2
                    pbank = C.bank(b)

                    def mm(e, wt=wt, s=s, ncol=ncol, pbank=pbank):
                        for k in range(NCH):
                            ins = e.matmul(pbank[:, 0:ncol], C.hnT[:, k, s * 128:(s + 1) * 128], wt[:, k, 0:ncol],
                                           start=(k == 0), stop=(k == NCH - 1))
                        return ins
                    S.op("pe", mm, reads=[("wi", slot)] + hn_keys, writes=[("bank", b)])
                    if grp < 5:
                        qs = nq_ev % 2
                        nq_ev += 1
                        stg = qrs[qs]
                        rope_evac(pbank.rearrange("p (h d) -> p h d", d=64), stg[:].rearrange("p (h d) -> p h d", d=64),
                                  blk, ("bank", b), ("qrs", qs))
                        tb = qs
                        pt = C.P["tr45"][tb]

                        def trq(e, pt=pt, stg=stg):
                            for j in range(4):
                                ins = e.transpose(pt[:, j * 128:(j + 1) * 128], stg[:, j * 128:(j + 1) * 128], C.ident[:])
                            return ins
                        S.op("pe", trq, reads=[("qrs", qs), "ident"], writes=[("bank", 4 + tb)])
                        pview = pt[:, 0:512].rearrange("p (g t) -> p g t", g=4)
                        if grp < 4:
                            S.op("act", lambda e, s=s, grp=grp, pview=pview: e.copy(
                                out=qT[:, grp * 4:(grp + 1) * 4, s * 128:(s + 1) * 128], in_=pview),
                                reads=[("bank", 4 + tb)], writes=[("qT", s, grp)])
                        else:
                            S.op("act", lambda e, s=s, pview=pview: e.copy(
                                out=kT2[:, :, (s + 1) * 128:(s + 2) * 128], in_=pview),
                                reads=[("bank", 4 + tb)], writes=[("kT2", s + 1)])
                    else:
                        pv = pbank[:, 0:256].rearrange("p (g d) -> p g d", d=64)
                        S.op("act", lambda e, s=s, pv=pv: e.copy(out=VAB[:, s + 1, :, 0, 0:64], in_=pv),
                             reads=[("bank", b), "VAB"], writes=[("VABa", s + 1)])
                        S.op("act", lambda e, s=s, pv=pv: e.copy(out=VAB[:, s + 1, :, 1, 64:128], in_=pv),
                             reads=[("bank", b), "VAB"], writes=[("VABb", s + 1)])
            if DBG in (1, 4, 5):
                continue
            asteps = [(c, kbk) for c in range(NCH) for kbk in range(0 if first else -1, 4)]
            kmin = 0 if first else -1

            def a_geom(i):
                c, kbk = asteps[i]
                g = c // 4
                nb, db = ((6, 7), (0, 1))[c % 2]
                slot = kbk + 1
                qlo = max(kbk, 0) * 128
                qhi = min(kbk + 2, 4) * 128
                ps = i % 2
                sb = 2 + 2 * ps
                stp = C.pA[:, 2:4, :] if ps == 0 else C.pB[:, 0:2, :]
                return c, kbk, g, nb, db, slot, qlo, qhi, qhi - qlo, ps, sb, stp

            def a_st(i):
                c, kbk, g, nb, db, slot, qlo, qhi, nq, ps, sb, stp = a_geom(i)
                qkeys = [("qT", s, c // 4) for s in range(4)]

                def mst(e):
                    for h in range(2):
                        ins = e.matmul(stp[:, h, 0:nq], kT2[h * 64:(h + 1) * 64, g, slot * 128:(slot + 1) * 128],
                                       qT[h * 64:(h + 1) * 64, c, qlo:qhi], start=True, stop=True)
                    return ins
                S.op("pe", mst, reads=[("kT2", slot)] + qkeys, writes=[("bank", sb), ("bank", sb + 1)])

            def a_rest(i, first=first):
                c, kbk, g, nb, db, slot, qlo, qhi, nq, ps, sb, stp = a_geom(i)
                num, den = C.bank(nb), C.bank(db)
                m0 = 128 if kbk == -1 else 0
                ptile = PT[ps]
                S.op("act", lambda e: e.activation(out=ptile[:, :, 0:nq], in_=stp[:, :, 0:nq], func=AF.Exp, scale=0.125),
                     reads=[("bank", sb), ("bank", sb + 1)], writes=[("PT", ps)])
                S.op("dve", lambda e: e.tensor_tensor(
                    out=ptile[:, :, 0:nq], in0=ptile[:, :, 0:nq],
                    in1=mask[:, m0:m0 + nq].unsqueeze(1).broadcast_to([128, 2, nq]), op=ALU.mult),
                    reads=[("PT", ps), "mask"], writes=[("PT", ps)])

                def mpv(e):
                    for j in range(nq // 128):
                        qb = qlo // 128 + j
                        stt = (kbk == qb - 1) or (kbk == qb and qb == 0 and first)
                        spp = (kbk == qb)
                        cs = slice(qb * 128, (qb + 1) * 128)
                        ps_ = slice(j * 128, (j + 1) * 128)
                        e.matmul(num[:, cs], VAB[:, slot, g, 0, :], ptile[:, 0, ps_], start=stt, stop=False)
                        e.matmul(num[:, cs], VAB[:, slot, g, 1, :], ptile[:, 1, ps_], start=False, stop=spp)
                        e.matmul(den[:, cs], DAB[:, 0, :], ptile[:, 0, ps_], start=stt, stop=False)
                        ins = e.matmul(den[:, cs], DAB[:, 1, :], ptile[:, 1, ps_], start=False, stop=spp)
                    return ins
                S.op("pe", mpv, reads=[("PT", ps), ("VABa", slot), ("VABb", slot), "VAB", "DAB"],
                     writes=[("bank", nb), ("bank", db)])
                if kbk == 3:
                    def fin():
                        S.op("dve", lambda e: e.tensor_scalar(out=dens[:], in0=den, scalar1=esink[:, c:c + 1],
                                                              scalar2=None, op0=ALU.add),
                             reads=[("bank", db), "esink"], writes=["dens"])
                        S.op("dve", lambda e: e.reciprocal(out=dens[:], in_=dens[:]), reads=["dens"], writes=["dens"])
                        S.op("dve", lambda e: e.tensor_tensor(out=oT[:, c, :], in0=num, in1=dens[:], op=ALU.mult),
                             reads=[("bank", nb), "dens"], writes=[("oT", c)])
                    apend.append((i + 2, fin))

            apend = []
            ALOOK = 1
            for i in range(len(asteps) + ALOOK):
                if i < len(asteps):
                    a_st(i)
                k = i - ALOOK
                if k >= 0:
                    a_rest(k)
                    while apend and (apend[0][0] <= k or k + 1 == len(asteps)):
                        apend.pop(0)[1]()
            if DBG == 2:
                continue
            if tt < ntile - 1:
                S.op("act", lambda e: e.copy(out=kT2[:, :, 0:128], in_=kT2[:, :, NT:NT + 128]),
                     reads=[("kT2", 4)], writes=[("kT2", 0)])
                S.op("act", lambda e: e.copy(out=VAB[:, 0], in_=VAB[:, 4]),
                     reads=[("VABa", 4), ("VABb", 4), "VAB"], writes=[("VABa", 0), ("VABb", 0)])
            C.out_proj(kb, S, src, dst, tt, oT, wout, 8, lambda pc, s: [("oT", pc * 8 + k) for k in range(8)], pipe_src=src)
        S.emit()


def fox_phase(kb, S, l, src, dst):
    nc = kb.nc
    T = kb.T
    ntile = T // NT
    nblk = T // 128
    NH = 32
    KA = 67
    win = kb.inputs["win%d" % l].rearrange("(c p) f -> p c f", p=128)
    wout = kb.inputs["wout%d" % l].rearrange("(c p) f -> p c f", p=128)

    def scr(nm, shape, dt):
        return nc.dram_tensor(kb.name(nm), list(shape), dt, kind="Internal").ap()
    QT = scr("QT", [NH, KA, T], BF16)
    KT = scr("KT", [NH, KA, T], BF16)
    GT = scr("GT", [NCH, 128, T], F32)
    VS = scr("VS", [T, D], BF16)
    OT = scr("OT", [NCH, 128, T], BF16)
    with ExitStack() as st0:
        cs_all = st0.enter_context(nc.sbuf_tensor(kb.name("cs_all"), [128, nblk, 32], F32))
        with ExitStack() as st:
            def A(nm, shape, dt):
                return st.enter_context(nc.sbuf_tensor(kb.name(nm), shape, dt))
            xs = [A("xs", [128, D], F32) for _ in range(2)]
            hnb = [A("hnb", [128, D], BF16) for _ in range(2)]
            junk = A("junk", [128, D], BF16)
            stat = A("stat", [128, 16], F32)
            hnT = A("hnT", [128, NCH, NT], BF16)
            pk = A("pk", [128, 512], F32)
            ident = A("ident", [128, 128], BF16)
            identf = A("identf", [128, 128], F32)
            epsc = A("epsc", [128, 1], F32)
            onec = A("onec", [128, 1], F32)
            GP = 2
            wi = [A("wi", [128, NCH, 3, GP * 128], BF16) for _ in range(2)]
            wv = [A("wv", [128, NCH, 512], BF16) for _ in range(2)]
            wf = A("wf", [128, NCH, 32], BF16)
            qst = [A("qst", [128, GP, NT], BF16) for _ in range(2)]
            kst = [A("kst", [128, GP, NT], BF16) for _ in range(2)]
            gst = [A("gst", [128, GP, NT], F32) for _ in range(2)]
            vst = A("vst", [128, 4, D], BF16)
            bfb = A("bfb", [128, 32], F32)
            zf = A("zf", [128, 32], F32)
            spf = A("spf", [128, 32], F32)
            cbase = A("cbase", [128, 32], F32)
            Utri = A("Utri", [128, 128], F32)
            ones = A("ones", [128, 128], F32)
            a32 = A("a32", [32, NT], F32)
            r32 = A("r32", [32, NT], F32)
            aq = A("aq", [32, 3, NT], BF16)
            ak = A("ak", [32, 3, NT], BF16)
            pA = st.enter_context(nc.psum_tensor(kb.name("pA"), [128, 4, 512], F32))
            pB = st.enter_context(nc.psum_tensor(kb.name("pB"), [128, 4, 512], F32))
            P = {"tr": [pB[:, 0, :].bitcast(BF16), pB[:, 1, :].bitcast(BF16)], "trbank": [4, 5]}
            gT = pk[:, 0:16]
            S.op("sp", lambda e: e.dma_start(out=pk[:], in_=kb.inputs["pk%d" % l]), writes=["gT", "pk"], dma="pk")
            S.op("sp", lambda e: e.dma_start(out=ident[:], in_=kb.inputs["ident"]), writes=["ident"], dma="ident")
            S.op("sp", lambda e: e.dma_start(out=identf[:], in_=kb.inputs["identf"]), writes=["identf"], dma="identf")
            S.op("sp", lambda e: e.dma_start(out=bfb[:], in_=kb.inputs["rows%d" % l][2, 0:32].partition_broadcast(128)),
                 writes=["bfb"], dma="bfb")
            S.op("sp", lambda e: e.dma_start(out=Utri[:], in_=kb.inputs["utri"]), writes=["utri"], dma="utri")
            S.op("pool", lambda e: e.dma_start(out=wf[:], in_=win[:, :, 6144:6176]), writes=["wf"], dma="wf")
            S.op("dve", lambda e: e.memset(epsc[:], EPS), writes=["epsc"])
            S.op("dve", lambda e: e.memset(onec[:], 1.0), writes=["onec"])
            S.op("dve", lambda e: e.memset(ones[:], 1.0), writes=["ones"])
            S.op("dve", lambda e: e.memset(cbase[:], 0.0), writes=["cbase"])
            S.op("dve", lambda e: e.memset(ak[:], 1.0), writes=["ak"])
            nwi = 0
            nwv = 0
            nst = 0
            for tt in range(ntile):
                _pre_norm_tile(kb, S, A, P, src, tt, gT, ident, hnT, xs, hnb, junk, stat, epsc)
                hn_keys = [("hnT", s, g) for s in range(4) for g in range(2)]
                c0t = tt * NT
                for grp in range(NCH // GP):
                    slot = nwi % 2
                    nwi += 1
                    wt = wi[slot]
                    for third, cbase_col in enumerate((0, 2048, 6176)):
                        c0 = cbase_col + grp * GP * 128
                        S.op("pool", lambda e, wt=wt, third=third, c0=c0: e.dma_start(
                            out=wt[:, :, third, :], in_=win[:, :, c0:c0 + GP * 128], max_dma_last_dim=8192),
                            writes=[("wi", slot, third)], dma="wi%d_%d" % (slot, third))
                    ss = nst % 2
                    nst += 1
                    for j in range(GP):
                        for third in range(3):
                            b = (j * 3 + third) % 4

                            def mm(e, wt=wt, third=third, j=j, b=b):
                                for k in range(NCH):
                                    ins = e.matmul(pA[:, b, :], wt[:, k, third, j * 128:(j + 1) * 128], hnT[:, k, :],
                                                   start=(k == 0), stop=(k == NCH - 1))
                                return ins
                            S.op("pe", mm, reads=[("wi", slot, third)] + hn_keys, writes=[("bank", b)])
                            if third == 0:
                                S.op("act", lambda e, b=b, ss=ss, j=j: e.copy(out=qst[ss][:, j, :], in_=pA[:, b, :]),
                                     reads=[("bank", b)], writes=[("qst", ss, j)])
                            elif third == 1:
                                S.op("dve", lambda e, b=b, ss=ss, j=j: e.tensor_copy(out=kst[ss][:, j, :], in_=pA[:, b, :]),
                                     reads=[("bank", b)], writes=[("kst", ss, j)])
                            else:
                                S.op("act", lambda e, b=b, ss=ss, j=j: e.activation(out=gst[ss][:, j, :], in_=pA[:, b, :],
                                                                                    func=AF.Sigmoid),
                                     reads=[("bank", b)], writes=[("gst", ss, j)])
                    cc = grp * GP
                    for j in range(GP):
                        for hh in range(2):
                            h = 2 * (cc + j) + hh
                            S.op("sp", lambda e, ss=ss, j=j, hh=hh, h=h, c0t=c0t: e.dma_start(
                                out=QT[h, 0:64, c0t:c0t + NT], in_=qst[ss][hh * 64:(hh + 1) * 64, j, :]),
                                reads=[("qst", ss, j)], dma="qst%d_%d" % (ss, j))
                            S.op("sp", lambda e, ss=ss, j=j, hh=hh, h=h, c0t=c0t: e.dma_start(
                                out=KT[h, 0:64, c0t:c0t + NT], in_=kst[ss][hh * 64:(hh + 1) * 64, j, :]),
                                reads=[("kst", ss, j)], dma="kst%d_%d" % (ss, j))
                    S.op("sp", lambda e, ss=ss, cc=cc, c0t=c0t: e.dma_start(
                        out=GT[cc:cc + GP, :, c0t:c0t + NT].rearrange("c p t -> p c t"), in_=gst[ss][:]),
                        reads=[("gst", ss, j) for j in range(GP)], dma="gst%d" % ss)
                for grp in range(4):
                    slot = nwv % 2
                    nwv += 1
                    wt = wv[slot]
                    c0 = 4096 + grp * 512
                    S.op("pool", lambda e, wt=wt, c0=c0: e.dma_start(out=wt[:], in_=win[:, :, c0:c0 + 512],
                                                                     max_dma_last_dim=8192),
                         writes=[("wv", slot)], dma="wv%d" % slot)
                    for s in range(4):
                        b = s % 2

                        def mmv(e, wt=wt, s=s, b=b):
                            for k in range(NCH):
                                ins = e.matmul(pA[:, b, :], hnT[:, k, s * 128:(s + 1) * 128], wt[:, k, :],
                                               start=(k == 0), stop=(k == NCH - 1))
                            return ins
                        S.op("pe", mmv, reads=[("wv", slot)] + hn_keys, writes=[("bank", b)])
                        S.op("act", lambda e, s=s, b=b, grp=grp: e.copy(out=vst[:, s, grp * 512:(grp + 1) * 512],
                                                                        in_=pA[:, b, :]),
                             reads=[("bank", b)], writes=[("vst", s)])
                for s in range(4):
                    r0 = c0t + s * 128
                    S.op("sp", lambda e, s=s, r0=r0: e.dma_start(out=VS[r0:r0 + 128, :], in_=vst[:, s, :]),
                         reads=[("vst", s)], dma="vst%d" % s)
                for s in range(4):
                    blk = tt * 4 + s

                    def mmf(e, s=s):
                        for k in range(NCH):
                            ins = e.matmul(pA[:, 2, 0:32], hnT[:, k, s * 128:(s + 1) * 128], wf[:, k, :],
                                           start=(k == 0), stop=(k == NCH - 1))
                        return ins
                    S.op("pe", mmf, reads=["wf"] + hn_keys, writes=[("bank", 2)])
                    S.op("dve", lambda e: e.tensor_tensor(out=zf[:], in0=pA[:, 2, 0:32], in1=bfb[:], op=ALU.add),
                         reads=[("bank", 2), "bfb"], writes=["zf"])
                    S.op("act", lambda e: e.activation(out=zf[:], in_=zf[:], func=AF.Exp, scale=-1.0),
                         reads=["zf"], writes=["zf"])
                    S.op("act", lambda e: e.activation(out=spf[:], in_=zf[:], func=AF.Ln, bias=onec[:, 0:1]),
                         reads=["zf", "onec"], writes=["spf"])

                    def mmc(e):
                        e.matmul(pA[:, 3, 0:32], Utri[:], spf[:], start=True, stop=True)
                        return e.matmul(pA[:, 3, 32:64], ones[:], spf[:], start=True, stop=True)
                    S.op("pe", mmc, reads=["utri", "ones", "spf"], writes=[("bank", 3)])
                    S.op("dve", lambda e, blk=blk: e.tensor_tensor(out=cs_all[:, blk, :], in0=pA[:, 3, 0:32], in1=cbase[:],
                                                                   op=ALU.add),
                         reads=[("bank", 3), "cbase"], writes=[("cs", blk)])
                    S.op("dve", lambda e: e.tensor_tensor(out=cbase[:], in0=pA[:, 3, 32:64], in1=cbase[:], op=ALU.add),
                         reads=[("bank", 3), "cbase"], writes=["cbase"])
                    S.op("pe", lambda e, blk=blk: e.transpose(pB[0:32, 2, 0:128], cs_all[:, blk, :], identf[:]),
                         reads=[("cs", blk), "identf"], writes=[("bank", 6)])
                    S.op("dve", lambda e, s=s: e.tensor_scalar(out=a32[:, s * 128:(s + 1) * 128], in0=pB[0:32, 2, 0:128],
                                                               scalar1=-8.0, scalar2=None, op0=ALU.mult),
                         reads=[("bank", 6)], writes=[("a32", s)])
                a_keys = [("a32", s) for s in range(4)]
                S.op("dve", lambda e: e.tensor_copy(out=aq[:, 0, :], in_=a32[:]), reads=a_keys, writes=["aq0"])
                S.op("dve", lambda e: e.tensor_tensor(out=r32[:], in0=a32[:], in1=aq[:, 0, :], op=ALU.subtract),
                     reads=a_keys + ["aq0"], writes=["r32"])
                S.op("dve", lambda e: e.tensor_copy(out=aq[:, 1, :], in_=r32[:]), reads=["r32"], writes=["aq1"])
                S.op("dve", lambda e: e.tensor_tensor(out=r32[:], in0=r32[:], in1=aq[:, 1, :], op=ALU.subtract),
                     reads=["r32", "aq1"], writes=["r32"])
                S.op("dve", lambda e: e.tensor_copy(out=aq[:, 2, :], in_=r32[:]), reads=["r32"], writes=["aq2"])
                S.op("sp", lambda e, c0t=c0t: e.dma_start(out=QT[:, 64:67, c0t:c0t + NT], in_=aq[:]),
                     reads=["aq0", "aq1", "aq2"], dma="aq")
                S.op("sp", lambda e, c0t=c0t: e.dma_start(out=KT[:, 64:67, c0t:c0t + NT], in_=ak[:]),
                     reads=["ak"], dma="ak")
            S.emit()
        with ExitStack() as st:
            def A(nm, shape, dt):
                return st.enter_context(nc.sbuf_tensor(kb.name(nm), shape, dt))
            Qh = [A("Qh", [128, T], BF16) for _ in range(2)]
            Kh = [A("Kh", [128, T], BF16) for _ in range(2)]
            Gh = [A("Gh", [64, T], F32) for _ in range(2)]
            Vh = [A("Vh", [128, nblk, 128], BF16) for _ in range(2)]
            od = [A("od", [128, NT], F32) for _ in range(2)]
            dl = [A("dl", [64, NT], F32) for _ in range(2)]
            tril = A("tril", [128, 128], BF16)
            PT = [A("PT", [128, NT], BF16) for _ in range(4)]
            rden = A("rden", [64, NT], F32)
            ofp = A("ofp", [64, NT], F32)
            ost = [A("ost", [64, NT], BF16) for _ in range(2)]
            pA = st.enter_context(nc.psum_tensor(kb.name("pA"), [128, 4, 512], F32))
            pB = st.enter_context(nc.psum_tensor(kb.name("pB"), [128, 4, 512], F32))
            S.op("sp", lambda e: e.dma_start(out=tril[:], in_=kb.inputs["swamask"][:, 0:128]), writes=["tril"], dma="tril")
            for i in range(2):
                S.op("dve", lambda e, i=i: e.memset(Vh[i][:], 1.0), writes=[("Vh1", i)])
                S.op("dve", lambda e, i=i: e.memset(Qh[i][:], 0.0), writes=[("Qh", i)])
                S.op("dve", lambda e, i=i: e.memset(Kh[i][:], 0.0), writes=[("Kh", i)])
            VSv = VS.rearrange("(b p) f -> p b f", p=128)
            NSB = 4
            LOOK = 2
            steps = []
            for h in range(NH):
                for Q in range(ntile):
                    for kbk in range(4 * Q + 4):
                        steps.append((h, Q, kbk))

            def emit_loads(h):
                emit_weight_convert(kb, S, l, h, NH)
                sl = h % 2
                c, hh = h // 2, h % 2
                S.op("sp", lambda e: e.dma_start(out=Qh[sl][0:KA, :], in_=QT[h]), writes=[("Qh", sl)], dma="Qh%d" % sl)
                S.op("sp", lambda e: e.dma_start(out=Kh[sl][0:KA, :], in_=KT[h]), writes=[("Kh", sl)], dma="Kh%d" % sl)
                S.op("sp", lambda e: e.dma_start(out=Gh[sl][:], in_=GT[c, hh * 64:(hh + 1) * 64, :]),
                     writes=[("Gh", sl)], dma="Gh%d" % sl)
                S.op("sp", lambda e: e.dma_start(out=Vh[sl][:, :, 0:64], in_=VSv[:, :, h * 64:(h + 1) * 64]),
                     reads=[("Vh1", sl)], writes=[("Vh", sl)], dma="Vh%d" % sl)

            def geom(i):
                h, Q, kbk = steps[i]
                j = kbk - 4 * Q
                q0 = max(j, 0) * 128
                return h, Q, kbk, j, q0, NT - q0, i % NSB

            def emit_st(i):
                h, Q, kbk, j, q0, nq, ps = geom(i)
                sl = h % 2
                qa = Q * NT + q0
                stp = pA[:, ps, :]
                S.op("pe", lambda e: e.matmul(stp[:, 0:nq], Kh[sl][:, kbk * 128:(kbk + 1) * 128],
                                              Qh[sl][:, qa:qa + nq], start=True, stop=True),
                     reads=[("Qh", sl), ("Kh", sl)], writes=[("bank", ps)])

            def emit_rest(i):
                h, Q, kbk, j, q0, nq, ps = geom(i)
                sl = h % 2
                c, hh = h // 2, h % 2
                fb = (h * ntile + Q) % 2
                pnd = pB[:, fb, :]
                nbk = 4 + fb
                nkb = 4 * Q + 4
                stp = pA[:, ps, :]
                ptile = PT[ps]
                S.op("act", lambda e: e.activation(out=ptile[:, 0:nq], in_=stp[:, 0:nq], func=AF.Exp, scale=0.125,
                                                   bias=cs_all[:, kbk, h:h + 1]),
                     reads=[("bank", ps)], writes=[("PT", ps)])
                if j >= 0:
                    S.op("dve", lambda e: e.tensor_tensor(out=ptile[:, 0:128], in0=ptile[:, 0:128], in1=tril[:],
                                                          op=ALU.mult),
                         reads=[("PT", ps), "tril"], writes=[("PT", ps)])
                S.op("pe", lambda e: e.matmul(pnd[:, q0:NT], Vh[sl][:, kbk, :], ptile[:, 0:nq],
                                              start=(kbk == 0), stop=(kbk == nkb - 1)),
                     reads=[("PT", ps), ("Vh", sl), ("Vh1", sl)], writes=[("bank", nbk)])
                if kbk == nkb - 1:
                    pending.append((i + 3, lambda: finalize(h, Q, fb, pnd, nbk, sl, c, hh)))

            def finalize(h, Q, fb, pnd, nbk, sl, c, hh):
                if True:
                    os_ = fb
                    S.op("act", lambda e: e.copy(out=od[os_][:], in_=pnd), reads=[("bank", nbk)], writes=[("od", os_)])
                    S.op("sp", lambda e: e.dma_start(out=dl[os_][:], in_=od[os_][64:128, :]),
                         reads=[("od", os_)], writes=[("dl", os_)], dma="dl%d" % os_)
                    S.op("dve", lambda e: e.reciprocal(out=dl[os_][:], in_=dl[os_][:]), reads=[("dl", os_)], writes=[("dl", os_)])
                    S.op("dve", lambda e: e.tensor_tensor(out=ofp[:], in0=od[os_][0:64, :], in1=dl[os_][:], op=ALU.mult),
                         reads=[("od", os_), ("dl", os_)], writes=["ofp"])
                    S.op("dve", lambda e: e.tensor_tensor(out=ost[os_][:], in0=ofp[:], in1=Gh[sl][:, Q * NT:(Q + 1) * NT],
                                                          op=ALU.mult),
                         reads=["ofp", ("Gh", sl)], writes=[("ost", os_)])
                    S.op("sp", lambda e: e.dma_start(out=OT[c, hh * 64:(hh + 1) * 64, Q * NT:(Q + 1) * NT], in_=ost[os_][:]),
                         reads=[("ost", os_)], dma="ost%d" % os_)

            pending = []
            emit_loads(0)
            emit_loads(1)
            for i in range(len(steps) + LOOK):
                if i < len(steps):
                    emit_st(i)
                k = i - LOOK
                if k >= 0:
                    emit_rest(k)
                    hk = steps[k][0]
                    if (k + 1 == len(steps) or steps[k + 1][0] != hk) and hk + 2 < NH:
                        pending.append((k + 3, lambda hk=hk: emit_loads(hk + 2)))
                    while pending and (pending[0][0] <= k or k + 1 == len(steps)):
                        pending.pop(0)[1]()
            S.emit()
    with ExitStack() as st:
        C = _Common(kb, S, st, l, 8)
        oT = C.A("oT", [128, NCH, NT], BF16)
        kb.wcnt = {"wo": 0}
        for tt in range(ntile):
            S.op("sp", lambda e, tt=tt: e.dma_start(out=oT[:], in_=OT[:, :, tt * NT:(tt + 1) * NT].rearrange("c p t -> p c t")),
                 writes=["oT"], dma="oTl")
            C.out_proj(kb, S, src, dst, tt, oT, wout, 8, lambda pc, s: ["oT"])
        S.emit()


def hgrn_phase(kb, S, l, src, dst):
    nc = kb.nc
    T = kb.T
    ntile = T // NT
    NHG = 16
    win = kb.inputs["win%d" % l].rearrange("(c p) f -> p c f", p=128)
    wout = kb.inputs["wout%d" % l].rearrange("(c p) f -> p c f", p=128)
    with ExitStack() as st:
        C = _Common(kb, S, st, l, 4)
        A = C.A
        pk = C.pk
        wi = [A("wi", [128, NCH, 4, 128], BF16) for _ in range(2)]
        oT = A("oT", [128, NCH, NT], BF16)
        Sst = A("Sst", [128, NHG, 128], F32)
        qt = [A("qt", [128, NT], BF16) for _ in range(3)]
        kt = [A("kt", [128, NT], BF16) for _ in range(3)]
        kh = [A("kh", [128, NT], BF16) for _ in range(3)]
        khT = [A("khT", [128, 4, 128], BF16) for _ in range(3)]
        vb = [A("vb", [128, 4, 128], BF16) for _ in range(3)]
        sgate = [A("sgate", [128, NT], F32) for _ in range(3)]
        esc = [A("esc", [128, 8], F32) for _ in range(3)]
        qs2 = [A("qs", [128, NT], F32) for _ in range(2)]
        ff2 = [A("ff", [128, NT], F32) for _ in range(2)]
        lf2 = [A("lf", [128, NT], F32)] * 2
        kk2 = [A("kk", [128, NT], F32) for _ in range(2)]
        bb2 = [A("bb", [128, NT], F32) for _ in range(2)]
        dd2 = [A("dd", [128, NT], F32)] * 2
        ee2 = [A("ee", [128, NT], F32)] * 2
        Sp = [A("Sp", [128, 128], BF16) for _ in range(8)]
        ATm = [A("ATm", [128, 128], BF16) for _ in range(8)]
        osb = A("osb", [128, NT], F32)
        sq = C.junk[:].bitcast(F32)[:, 0:NT]
        rr_ = C.junk[:].bitcast(F32)[:, NT:2 * NT]
        smask = A("smask", [128, NT], F32)
        tril = A("tril", [128, 128], BF16)
        onesf = A("onesf", [128, 128], F32)
        lbe = A("lbe", [128, NCH, 5], F32)
        lbs = A("lbs", [128, NCH], F32)
        lbv = A("lbv", [128, NCH], F32)
        oml = A("oml", [128, NCH], F32)
        epsc = C.epsc
        ng = pk[:, 464:465]
        S.op("sp", lambda e: e.dma_start(out=tril[:], in_=kb.inputs["swamask"][:, 0:128]), writes=["tril"], dma="tril")
        S.op("dve", lambda e: e.memset(onesf[:], 1.0), writes=["onesf"])
        for i8 in range(8):
            S.op("dve", lambda e, i8=i8: e.memset(ATm[i8][:], 0.0), writes=["ATz"])
        S.op("dve", lambda e: e.memset(Sst[:], 0.0), writes=[("S", h) for h in range(NHG)])
        S.op("dve", lambda e: e.memset(smask[:], 1.0), writes=["smask"])
        S.op("dve", lambda e: e.memset(smask[:].rearrange("p (c t) -> p c t", t=128)[:, :, 0:1], 0.0),
             reads=["smask"], writes=["smask"])
        S.op("act", lambda e: e.activation(out=lbe[:], in_=pk[:, 384:464].rearrange("p (c i) -> p c i", i=5), func=AF.Exp),
             reads=["pk"], writes=["lbe"])
        S.op("dve", lambda e: e.tensor_reduce(out=lbs[:], in_=lbe[:], axis=AX.X, op=ALU.add), reads=["lbe"], writes=["lbs"])
        S.op("dve", lambda e: e.reciprocal(out=lbs[:], in_=lbs[:]), reads=["lbs"], writes=["lbs"])
        S.op("dve", lambda e: e.tensor_reduce(out=lbv[:], in_=lbe[:, :, 0:l + 1], axis=AX.X, op=ALU.add),
             reads=["lbe"], writes=["lbv"])
        S.op("dve", lambda e: e.tensor_tensor(out=lbv[:], in0=lbv[:], in1=lbs[:], op=ALU.mult),
             reads=["lbv", "lbs"], writes=["lbv"])
        S.op("dve", lambda e: e.tensor_scalar(out=oml[:], in0=lbv[:], scalar1=-1.0, scalar2=1.0, op0=ALU.mult, op1=ALU.add),
             reads=["lbv"], writes=["oml"])
        kb.wcnt = {"wi": 0, "wo": 0}
        pq, pf, pg, pv = C.bank(0), C.bank(1), C.bank(2), C.bank(3)
        pAT = C.bank(4).rearrange("p (j t) -> p j t", j=4)
        po = C.bank(5)
        pU = C.bank(6).rearrange("p (j t) -> p j t", j=4)
        pss = C.bank(7)
        ptr = C.bank(7).bitcast(BF16)

        def proj(tt, h, hn_keys):
            par = h % 3
            tb = h % 2
            qs, ff, lf, kk, bb, dd, ee = qs2[tb], ff2[tb], lf2[tb], kk2[tb], bb2[tb], dd2[tb], ee2[tb]
            kq, kf, kl, kkk, kb_, kd, ke = [(n, tb) for n in ("qs", "ff", "lf", "kk", "bb", "dd", "ee")]
            kd, kl, ke = "dd", "lf", "ee"
            slot = kb.wcnt["wi"] % 2
            kb.wcnt["wi"] += 1
            wt = wi[slot]
            for q4 in range(4):
                c0 = q4 * D + h * 128
                S.op("pool", lambda e, wt=wt, q4=q4, c0=c0: e.dma_start(out=wt[:, :, q4, :], in_=win[:, :, c0:c0 + 128],
                                                                        max_dma_last_dim=8192),
                     writes=[("wi", slot, q4)], dma="wi%d_%d" % (slot, q4))
            for q4, pb, bk in ((0, pq, 0), (1, pf, 1), (3, pg, 2)):
                def mm(e, wt=wt, q4=q4, pb=pb):
                    for k in range(NCH):
                        ins = e.matmul(pb, wt[:, k, q4, :], C.hnT[:, k, :], start=(k == 0), stop=(k == NCH - 1))
                    return ins
                S.op("pe", mm, reads=[("wi", slot, q4)] + hn_keys, writes=[("bank", bk)])

            def mmv(e, wt=wt):
                for s in range(4):
                    for k in range(NCH):
                        ins = e.matmul(pv[:, s * 128:(s + 1) * 128], C.hnT[:, k, s * 128:(s + 1) * 128], wt[:, k, 2, :],
                                       start=(k == 0), stop=(k == NCH - 1))
                return ins
            S.op("pe", mmv, reads=[("wi", slot, 2)] + hn_keys, writes=[("bank", 3)])
            S.op("act", lambda e: e.copy(out=vb[par][:], in_=pv.rearrange("p (s d) -> p s d", s=4)),
                 reads=[("bank", 3)], writes=[("vb", par)])
            S.op("act", lambda e: e.activation(out=qs[:], in_=pq, func=AF.Silu), reads=[("bank", 0)], writes=[kq])
            S.op("act", lambda e: e.activation(out=ff[:], in_=pf, func=AF.Sigmoid), reads=[("bank", 1)], writes=[kf])
            S.op("act", lambda e: e.activation(out=sgate[par][:], in_=pg, func=AF.Silu), reads=[("bank", 2)],
                 writes=[("sgate", par)])
            S.op("dve", lambda e: e.tensor_scalar(out=ff[:], in0=ff[:], scalar1=oml[:, h:h + 1], scalar2=lbv[:, h:h + 1],
                                                  op0=ALU.mult, op1=ALU.add), reads=[kf, "oml", "lbv"], writes=[kf])
            S.op("act", lambda e: e.activation(out=lf[:], in_=ff[:], func=AF.Ln), reads=[kf], writes=[kl])
            S.op("dve", lambda e: e.tensor_scalar(out=kk[:], in0=ff[:], scalar1=-1.0, scalar2=1.0, op0=ALU.mult, op1=ALU.add),
                 reads=[kf], writes=[kkk])
            S.op("dve", lambda e: e.tensor_tensor_scan(out=bb[:], data0=smask[:], data1=lf[:], initial=0.0,
                                                       op0=ALU.mult, op1=ALU.add),
                 reads=["smask", kl], writes=[kb_])
            b3 = bb[:].rearrange("p (c t) -> p c t", t=128)
            S.op("dve", lambda e: e.tensor_tensor(out=dd[:].rearrange("p (c t) -> p c t", t=128), in0=b3,
                                                  in1=b3[:, :, 63:64].broadcast_to([128, 4, 128]), op=ALU.subtract),
                 reads=[kb_], writes=[kd])
            S.op("act", lambda e: e.activation(out=ee[:], in_=dd[:], func=AF.Exp), reads=[kd], writes=[ke])
            S.op("dve", lambda e: e.tensor_tensor(out=qt[par][:], in0=qs[:], in1=ee[:], op=ALU.mult),
                 reads=[kq, ke], writes=[("qt", par)])
            S.op("act", lambda e: e.activation(out=ee[:], in_=dd[:], func=AF.Exp, scale=-1.0), reads=[kd], writes=[ke])
            S.op("dve", lambda e: e.tensor_tensor(out=kt[par][:], in0=kk[:], in1=ee[:], op=ALU.mult),
                 reads=[kkk, ke], writes=[("kt", par)])
            S.op("dve", lambda e: e.tensor_tensor(out=dd[:].rearrange("p (c t) -> p c t", t=128), in0=b3,
                                                  in1=b3[:, :, 127:128].broadcast_to([128, 4, 128]), op=ALU.subtract),
                 reads=[kb_], writes=[kd])
            S.op("act", lambda e: e.activation(out=ee[:], in_=dd[:], func=AF.Exp, scale=-1.0), reads=[kd], writes=[ke])
            S.op("dve", lambda e: e.tensor_tensor(out=kh[par][:], in0=kk[:], in1=ee[:], op=ALU.mult),
                 reads=[kkk, ke], writes=[("kh", par)])
            S.op("act", lambda e: e.activation(out=esc[par][:, 0:4], in_=b3[:, :, 127], func=AF.Exp), reads=[kb_],
                 writes=[("esc", par, 0)])
            S.op("act", lambda e: e.activation(out=esc[par][:, 4:8], in_=b3[:, :, 63], func=AF.Exp), reads=[kb_],
                 writes=[("esc", par, 1)])

        def chainA(tt, h):
            par = h % 3
            hp = h % 2

            def trk(e):
                for j in range(4):
                    ins = e.transpose(ptr[:, j * 128:(j + 1) * 128], kh[par][:, j * 128:(j + 1) * 128], C.ident[:])
                return ins
            S.op("pe", trk, reads=[("kh", par), "ident"], writes=[("bank", 7)])
            S.op("dve", lambda e: e.tensor_copy(out=khT[par][:], in_=ptr[:, 0:512].rearrange("p (j t) -> p j t", j=4)),
                 reads=[("bank", 7)], writes=[("khT", par)])

            def mat(e):
                for j in range(4):
                    c0 = j * 128
                    e.matmul(pAT[0:64, j, 0:64], kt[par][:, c0:c0 + 64], qt[par][:, c0:c0 + 64], start=True, stop=True)
                    ins = e.matmul(pAT[:, j, 64:128], kt[par][:, c0:c0 + 128], qt[par][:, c0 + 64:c0 + 128],
                                   start=True, stop=True)
                return ins
            S.op("pe", mat, reads=[("kt", par), ("qt", par)], writes=[("bank", 4)])

            def mu(e):
                for j in range(4):
                    ins = e.matmul(pU[:, j, :], khT[par][:, j, :], vb[par][:, j, :], start=True, stop=True)
                return ins
            S.op("pe", mu, reads=[("khT", par), ("vb", par)], writes=[("bank", 6)])
            for j in range(4):
                at = ATm[hp * 4 + j]
                S.op("dve", lambda e, at=at, j=j: e.tensor_tensor(out=at[0:64, 0:64], in0=pAT[0:64, j, 0:64],
                                                                  in1=tril[0:64, 0:64], op=ALU.mult),
                     reads=[("bank", 4), "tril", "ATz"], writes=[("ATm", hp, j)])
                S.op("dve", lambda e, at=at, j=j: e.tensor_tensor(out=at[:, 64:128], in0=pAT[:, j, 64:128],
                                                                  in1=tril[:, 64:128], op=ALU.mult),
                     reads=[("bank", 4), "tril", "ATz"], writes=[("ATm", hp, j)])
            for j in range(4):
                sp = Sp[hp * 4 + j]
                S.op("dve", lambda e, sp=sp, j=j: e.tensor_scalar(out=sp[:], in0=Sst[:, h, :], scalar1=esc[par][:, 4 + j:5 + j],
                                                                  scalar2=None, op0=ALU.mult),
                     reads=[("S", h), ("esc", par, 1)], writes=[("Sp", hp, j)])
                S.op("dve", lambda e, j=j: e.scalar_tensor_tensor(out=Sst[:, h, :], in0=Sst[:, h, :],
                                                                  scalar=esc[par][:, j:j + 1], in1=pU[:, j, :],
                                                                  op0=ALU.mult, op1=ALU.add),
                     reads=[("S", h), ("esc", par, 0), ("bank", 6)], writes=[("S", h)])

        def chainC(tt, h):
            par = h % 3
            hp = h % 2

            def mo(e):
                for j in range(4):
                    cs = slice(j * 128, (j + 1) * 128)
                    e.matmul(po[:, cs], Sp[hp * 4 + j][:], qt[par][:, cs], start=True, stop=False)
                    ins = e.matmul(po[:, cs], vb[par][:, j, :], ATm[hp * 4 + j][:], start=False, stop=True)
                return ins
            S.op("pe", mo, reads=[("Sp", hp, j) for j in range(4)] + [("ATm", hp, j) for j in range(4)] +
                 [("qt", par), ("vb", par)], writes=[("bank", 5)])
            S.op("act", lambda e: e.copy(out=osb[:], in_=po), reads=[("bank", 5)], writes=["osb"])
            S.op("act", lambda e: e.activation(out=sq, in_=osb[:], func=AF.Square), reads=["osb"], writes=["junk"])
            S.op("pe", lambda e: e.matmul(pss, onesf[:], sq, start=True, stop=True), reads=["onesf", "junk"],
                 writes=[("bank", 7)])
            S.op("act", lambda e: e.activation(out=rr_, in_=pss, func=AF.Ln, scale=1.0 / 128, bias=epsc[:, 0:1]),
                 reads=[("bank", 7), "epsc"], writes=["rr"])
            S.op("act", lambda e: e.activation(out=rr_, in_=rr_, func=AF.Exp, scale=-0.5), reads=["rr"], writes=["rr"])
            S.op("dve", lambda e: e.tensor_tensor(out=osb[:], in0=osb[:], in1=rr_, op=ALU.mult),
                 reads=["osb", "rr"], writes=["osb"])
            S.op("dve", lambda e: e.scalar_tensor_tensor(out=oT[:, h, :], in0=osb[:], scalar=ng, in1=sgate[par][:],
                                                         op0=ALU.mult, op1=ALU.mult),
                 reads=["osb", "pk", ("sgate", par)], writes=[("oT", h)])

        for tt in range(ntile):
            emit_weight_convert(kb, S, l, tt, ntile)
            hn_keys = C.pre_norm(kb, S, src, tt)
            for i in range(NHG + 2):
                if i < NHG:
                    proj(tt, i, hn_keys)
                if 0 <= i - 1 < NHG:
                    chainA(tt, i - 1)
                if 0 <= i - 2 < NHG:
                    chainC(tt, i - 2)
            C.out_proj(kb, S, src, dst, tt, oT, wout, 4, lambda pc, s: [("oT", pc * 4 + k) for k in range(4)], pipe_src=src)
        S.emit()


def mixer_phase(kb, S, l, m, src, dst):
    if m == 2:
        sconv_phase(kb, S, l, src, dst)
    elif m == 1:
        swa_phase(kb, S, l, src, dst)
    elif m == 3:
        fox_phase(kb, S, l, src, dst)
    elif m == 0:
        hgrn_phase(kb, S, l, src, dst)
    else:
        raise NotImplementedError(m)


def swa_mask():
    k = np.arange(128)[:, None]
    q = np.arange(128)[None, :]
    return np.concatenate([(k <= q), (k > q)], axis=1).astype(ml_dtypes.bfloat16)


def mixer_inputs(l, inp, m, j, T):
    out = {}
    if m == 2:
        out["win%d" % l] = np.asarray(inp["sc_w_in"][j])
        out["wout%d" % l] = np.asarray(inp["sc_w_out"][j])
    elif m == 1:
        w = np.asarray(inp["swa_w_in"][j])
        wq, wk, wv = w[:, :2048], w[:, 2048:2304], w[:, 2304:2560]
        wk2 = np.concatenate([np.concatenate([wk[:, g * 64:(g + 1) * 64]] * 2, axis=1) for g in range(4)], axis=1)
        out["win%d" % l] = np.ascontiguousarray(np.concatenate([wq, wk2, wv], axis=1))
        out["wout%d" % l] = np.asarray(inp["swa_w_out"][j])
        out["posT"] = np.ascontiguousarray(np.asarray(inp["positions"]).astype(np.int32).reshape(-1, 128).T)
        out["invf"] = np.tile((np.float32(500000.0) ** (-(np.arange(8, dtype=np.float32)) / 8)).astype(np.float32), (128, 1))
        out["swamask"] = swa_mask()
    elif m == 0:
        out["win%d" % l] = np.asarray(inp["hgrn_w_in"][j])
        out["wout%d" % l] = np.asarray(inp["hgrn_w_out"][j])
        out["swamask"] = swa_mask()
    elif m == 3:
        out["win%d" % l] = np.asarray(inp["fox_w_in"][j])
        out["wout%d" % l] = np.asarray(inp["fox_w_out"][j])
        out["swamask"] = swa_mask()
        out["utri"] = np.triu(np.ones((128, 128), np.float32))
        out["identf"] = np.eye(128, dtype=np.float32)
    return out


_CACHE = {}


def make_in_maps(inputs, T, layers, ncores):
    inp = {k: np.asarray(v) for k, v in inputs.items()}
    shared = {"ident": np.eye(128).astype(ml_dtypes.bfloat16)}
    for l, m in enumerate(layers):
        j = l // 4
        pk, rows = pack_layer(l, inp, m, j)
        shared["pk%d" % l] = pk
        shared["rows%d" % l] = rows
        shared["wup%d" % l] = inp["ffn_w_up"][l]
        shared["wdn%d" % l] = inp["ffn_w_down"][l]
        shared.update(mixer_inputs(l, inp, m, j, T))
    x = inp["x"]
    maps = []
    for c in range(ncores):
        d = dict(shared)
        d["x"] = np.ascontiguousarray(x[c])
        maps.append(d)
    return maps


def kernel(**inputs):
    x = np.asarray(inputs["x"])
    B, T, _ = x.shape
    layers = [i % 4 for i in range(DEPTH)]
    key = (T, tuple(layers))
    if key not in _CACHE:
        _CACHE[key] = build(T, layers)
    kb = _CACHE[key]
    maps = make_in_maps(inputs, T, layers, B)
    maps = [{k: v for k, v in m.items() if k in kb.inputs} for m in maps]
    res = run_bass_kernel_spmd(kb.nc, maps, core_ids=list(range(B)))
    return np.stack([np.asarray(r["y"]) for r in res.results], axis=0).astype(np.float32)
```

```python
import numpy as np
import ml_dtypes
import concourse.bass as bass
import concourse.mybir as mybir
from concourse.bass_utils import run_bass_kernel_spmd
from contextlib import ExitStack

F32 = mybir.dt.float32
BF16 = mybir.dt.bfloat16
AF = mybir.ActivationFunctionType
ALU = mybir.AluOpType
AX = mybir.AxisListType

D = 2048
DFF = 5632
NCH = D // 128
FCH = DFF // 128
EPS = 1e-6
NT = 512
NCORES = 8
SEQ = 4096
DEPTH = 4
WIN_COLS = {0: 8192, 1: 2816, 2: 6144, 3: 8224}


class _Op:
    __slots__ = ("eng", "fn", "deps", "dma", "sig", "idx")

    def __init__(self, eng, fn, deps, dma):
        self.eng = eng
        self.fn = fn
        self.deps = deps
        self.dma = dma
        self.sig = False
        self.idx = 0


class Sched:
    ENGS = ("pe", "act", "dve", "pool", "sp")

    def __init__(self, nc, stack):
        self.nc = nc
        self.stack = stack
        self.sems = {}
        self.cnt = {}
        self.bar = stack.enter_context(nc.semaphore("bar"))
        self.nbar = 0
        self.begin()

    def begin(self):
        self.ops = []
        self.W = {}
        self.R = {}

    def _sem(self, key):
        if key not in self.sems:
            self.sems[key] = self.stack.enter_context(self.nc.semaphore("s%d" % len(self.sems)))
            self.cnt[key] = 0
        return self.sems[key]

    def op(self, eng, fn, reads=(), writes=(), dma=None):
        idx = len(self.ops)
        deps = set()
        bank_reads = [t for t in reads if isinstance(t, tuple) and t[0] == "bank"]
        if bank_reads:
            reads = [t for t in reads if t not in bank_reads]
            writes = list(writes) + [t for t in bank_reads if t not in writes]
        for t in reads:
            w = self.W.get(t)
            if w:
                deps |= w
        for t in writes:
            w = self.W.get(t)
            if w:
                deps |= w
            r = self.R.get(t)
            if r:
                for k, v in r.items():
                    if k is None:
                        deps.update(v)
                    else:
                        deps.add(v)
        for t in writes:
            r = self.R.get(t)
            if r:
                self.W[t] = {idx}
                self.R[t] = {}
            else:
                self.W.setdefault(t, set()).add(idx)
        for t in reads:
            r = self.R.setdefault(t, {})
            if dma is not None:
                r.setdefault(None, []).append(idx)
            else:
                r[eng] = idx
        deps.discard(idx)
        self.ops.append(_Op(eng, fn, deps, dma))
        return idx

    def emit(self):
        nc = self.nc
        ops = self.ops
        per_eng = {e: [] for e in self.ENGS}
        for i, o in enumerate(ops):
            per_eng[o.eng].append(i)
            for d in o.deps:
                ops[d].sig = True
        for e in self.ENGS:
            lst = [i for i in per_eng[e] if ops[i].dma is None]
            if lst:
                ops[lst[-1]].sig = True
        used = []
        for o in ops:
            key = ("dma", o.dma) if o.dma is not None else ("eng", o.eng)
            if o.dma is not None or o.sig:
                self._sem(key)
                if key not in used:
                    used.append(key)
                self.cnt[key] += 1
                o.idx = self.cnt[key]
        sems = self.sems
        final = {k: self.cnt[k] for k in used}
        self.nbar += 1
        nbar = self.nbar
        bar = self.bar

        def run_engine(e, engobj):
            seen = {}
            for i in per_eng[e]:
                o = ops[i]
                need = {}
                for d in o.deps:
                    p = ops[d]
                    if p.dma is not None:
                        key = ("dma", p.dma)
                        val = 16 * p.idx
                    else:
                        key = ("eng", p.eng)
                        val = p.idx
                    if need.get(key, 0) < val:
                        need[key] = val
                for key, val in need.items():
                    if seen.get(key, 0) < val:
                        engobj.wait_ge(sems[key], val)
                        seen[key] = val
                ins = o.fn(engobj)
                if o.dma is not None:
                    ins.then_inc(sems[("dma", o.dma)], 16)
                elif o.sig:
                    ins.then_inc(sems[("eng", o.eng)], 1)
            if e == "sp":
                for key, v in final.items():
                    engobj.wait_ge(sems[key], 16 * v if key[0] == "dma" else v)
                engobj.sem_inc(bar, 1)
            else:
                engobj.wait_ge(bar, nbar)

        with nc.Block() as block:
            @block.tensor
            def _(eng):
                run_engine("pe", eng)

            @block.scalar
            def _(eng):
                run_engine("act", eng)

            @block.vector
            def _(eng):
                run_engine("dve", eng)

            @block.gpsimd
            def _(eng):
                run_engine("pool", eng)

            @block.sync
            def _(eng):
                run_engine("sp", eng)
        self.begin()


class KB:
    def __init__(self, T, layers):
        self.T = T
        self.layers = layers
        self.nc = bass.Bass("TRN2", target_bir_lowering=False)
        self.uid = 0
        self.inputs = {}

    def din(self, name, shape, dt=F32):
        t = self.nc.dram_tensor(name, list(shape), dt, kind="ExternalInput").ap()
        self.inputs[name] = t
        return t

    def name(self, s):
        self.uid += 1
        return "%s_%d" % (s, self.uid)


def _pre_norm_tile(kb, S, A, P, src, tt, gT, ident, hnT, xs, hnb, junk, stat, epsc, subs=None):
    nsub = NT // 128
    for s in (range(nsub) if subs is None else subs):
        r0 = tt * NT + s * 128
        sl = s % 2
        S.op("sp", lambda e, sl=sl, r0=r0: e.dma_start(out=xs[sl][:], in_=src[r0:r0 + 128, :]),
             writes=[("xs", sl)], dma="xs%d" % sl)
        S.op("act", lambda e, sl=sl: e.activation(out=junk[:], in_=xs[sl][:], func=AF.Square,
                                                  accum_out=stat[:, sl:sl + 1]),
             reads=[("xs", sl)], writes=["junk", ("ss", sl)])
        S.op("act", lambda e, sl=sl: e.activation(out=stat[:, 2 + sl:3 + sl], in_=stat[:, sl:sl + 1],
                                                  func=AF.Sqrt, scale=1.0 / D, bias=epsc[:, 0:1]),
             reads=[("ss", sl), "epsc"], writes=[("ms", sl)])
        S.op("dve", lambda e, sl=sl: e.reciprocal(out=stat[:, 4 + sl:5 + sl], in_=stat[:, 2 + sl:3 + sl]),
             reads=[("ms", sl)], writes=[("rstd", sl)])
        S.op("dve", lambda e, sl=sl: e.tensor_scalar(out=hnb[sl][:], in0=xs[sl][:], scalar1=stat[:, 4 + sl:5 + sl],
                                                     scalar2=None, op0=ALU.mult),
             reads=[("xs", sl), ("rstd", sl)], writes=[("hnb", sl)])
        for g in range(2):
            pt = P["tr"][g]

            def tr(e, g=g, sl=sl, pt=pt):
                for j in range(8):
                    c = g * 8 + j
                    ins = e.transpose(pt[:, j * 128:(j + 1) * 128], hnb[sl][:, c * 128:(c + 1) * 128], ident[:])
                return ins
            S.op("pe", tr, reads=[("hnb", sl), "ident"], writes=[("bank", P["trbank"][g])])
            S.op("dve", lambda e, g=g, s=s, pt=pt: e.tensor_tensor(
                out=hnT[:, g * 8:(g + 1) * 8, s * 128:(s + 1) * 128],
                in0=pt.rearrange("p (c t) -> p c t", c=8),
                in1=gT[:, g * 8:(g + 1) * 8].unsqueeze(2).broadcast_to([128, 8, 128]), op=ALU.mult),
                reads=[("bank", P["trbank"][g]), "gT"], writes=[("hnT", s, g)])


def _post_norm_residual(kb, S, ybuf, s, src, dst, r0, grow, xs, junk, stat, ykey, epsc):
    sl = s % 2
    S.op("sp", lambda e: e.dma_start(out=xs[sl][:], in_=src[r0:r0 + 128, :]),
         writes=[("xs", sl)], dma="xs%d" % sl)
    S.op("act", lambda e: e.activation(out=junk[:], in_=ybuf[:, s, :], func=AF.Square,
                                       accum_out=stat[:, 6 + sl:7 + sl]),
         reads=[ykey], writes=["junk", ("pss", sl)])
    S.op("act", lambda e: e.activation(out=stat[:, 8 + sl:9 + sl], in_=stat[:, 6 + sl:7 + sl],
                                       func=AF.Sqrt, scale=1.0 / D, bias=epsc[:, 0:1]),
         reads=[("pss", sl), "epsc"], writes=[("pms", sl)])
    S.op("dve", lambda e: e.reciprocal(out=stat[:, 10 + sl:11 + sl], in_=stat[:, 8 + sl:9 + sl]),
         reads=[("pms", sl)], writes=[("prstd", sl)])
    S.op("dve", lambda e: e.scalar_tensor_tensor(out=ybuf[:, s, :], in0=ybuf[:, s, :],
                                                 scalar=stat[:, 10 + sl:11 + sl], in1=grow[:],
                                                 op0=ALU.mult, op1=ALU.mult),
         reads=[ykey, ("prstd", sl), "grow"], writes=[ykey])
    S.op("dve", lambda e: e.tensor_tensor(out=xs[sl][:], in0=xs[sl][:], in1=ybuf[:, s, :], op=ALU.add),
         reads=[ykey, ("xs", sl)], writes=[("xs", sl)])
    S.op("sp", lambda e: e.dma_start(out=dst[r0:r0 + 128, :], in_=xs[sl][:]),
         reads=[("xs", sl)], writes=[("dram", id(dst.tensor), r0)], dma="xs%d" % sl)


def _out_proj_tile(kb, S, P, lhs_fn, nk, wsrc, wslots, wname, kpiece, ybuf, lhs_keys, after_dc=None):
    nsub = NT // 128
    npiece = nk // kpiece
    cnt = kb.wcnt
    for dc in range(4):
        for pc in range(npiece):
            slot = cnt[wname] % 2
            cnt[wname] += 1
            wt = wslots[slot]
            S.op("pool", lambda e, wt=wt, pc=pc, dc=dc: e.dma_start(
                out=wt[:, 0:kpiece, :], in_=wsrc[:, pc * kpiece:(pc + 1) * kpiece, dc * 512:(dc + 1) * 512],
                max_dma_last_dim=2048),
                writes=[(wname, slot)], dma="%s%d" % (wname, slot))
            for s in range(nsub):
                def mm(e, wt=wt, pc=pc, s=s):
                    for k in range(kpiece):
                        kk = pc * kpiece + k
                        ins = e.matmul(P["dn"][s][:], lhs_fn(kk, s), wt[:, k, :],
                                       start=(kk == 0), stop=(kk == nk - 1))
                    return ins
                S.op("pe", mm, reads=[(wname, slot)] + lhs_keys(pc, s), writes=[("bank", P["dnbank"][s])])
        for s in range(nsub):
            S.op("act", lambda e, s=s, dc=dc: e.copy(out=ybuf[:, s, dc * 512:(dc + 1) * 512], in_=P["dn"][s][:]),
                 reads=[("bank", P["dnbank"][s])], writes=[("ybuf", s)])
        if after_dc is not None:
            after_dc(dc)


def ffn_phase(kb, S, l, src, dst):
    nc = kb.nc
    T = kb.T
    ntile = T // NT
    nsub = NT // 128
    GP = 2
    wup = kb.inputs["wup%d" % l].rearrange("(c p) f -> p c f", p=128)
    wdn = kb.inputs["wdn%d" % l].rearrange("(c p) f -> p c f", p=128)
    pk_d = kb.inputs["pk%d" % l]
    rows_d = kb.inputs["rows%d" % l]
    with ExitStack() as st:
        def A(nm, shape, dt):
            return st.enter_context(nc.sbuf_tensor(kb.name(nm), shape, dt))
        xs = [A("xs", [128, D], F32) for _ in range(2)]
        hnb = [A("hnb", [128, D], BF16) for _ in range(2)]
        junk = A("junk", [128, D], BF16)
        stat = A("stat", [128, 16], F32)
        hnT = A("hnT", [128, NCH, NT], BF16)
        gbuf = A("g", [128, FCH, NT], BF16)
        wu = [A("wu", [128, NCH, 2, GP * 128], BF16) for _ in range(2)]
        wd = [A("wd", [128, 11, 512], BF16) for _ in range(2)]
        ybuf = A("ybuf", [128, nsub, D], F32)
        grow = A("grow", [128, D], F32)
        pk = A("pk", [128, 512], F32)
        ident = A("ident", [128, 128], BF16)
        ub = [A("ub", [128, NT + 2], F32) for _ in range(2)]
        acc = [A("acc", [128, NT], F32) for _ in range(2)]
        sg = A("sg", [128, NT], F32)
        halo = A("halo", [128, 2 * FCH, 2], F32)
        pup = st.enter_context(nc.psum_tensor(kb.name("pup"), [128, 4, 512], F32))
        pdn = st.enter_context(nc.psum_tensor(kb.name("pdn"), [128, 4, 512], F32))
        P = {"tr": [pup[:, 0, :].bitcast(BF16), pup[:, 1, :].bitcast(BF16)], "trbank": [0, 1],
             "dn": [pdn[:, s, :] for s in range(4)], "dnbank": [4, 5, 6, 7]}
        gT = pk[:, 16:32]
        cw = pk[:, 32:32 + 2 * FCH * 3].rearrange("p (c k) -> p c k", k=3)
        cb = pk[:, 296:296 + 2 * FCH]

        S.op("sp", lambda e: e.dma_start(out=pk[:], in_=pk_d), writes=["gT", "pk"], dma="pk")
        S.op("sp", lambda e: e.dma_start(out=grow[:], in_=rows_d[1].partition_broadcast(128)),
             writes=["grow"], dma="grow")
        S.op("sp", lambda e: e.dma_start(out=ident[:], in_=kb.inputs["ident"]), writes=["ident"], dma="ident")
        S.op("dve", lambda e: e.memset(halo[:], 0.0), writes=["halo"])
        epsc = A("epsc", [128, 1], F32)
        S.op("dve", lambda e: e.memset(epsc[:], EPS), writes=["epsc"])
        kb.wcnt = {"wu": 0, "wd": 0}
        _pre_norm_tile(kb, S, A, P, src, 0, gT, ident, hnT, xs, hnb, junk, stat, epsc)
        for tt in range(ntile):
            hn_keys = [("hnT", s, g) for s in range(nsub) for g in range(2)]
            for grp in range(FCH // GP):
                slot = kb.wcnt["wu"] % 2
                kb.wcnt["wu"] += 1
                wt = wu[slot]
                for half in range(2):
                    c0 = half * DFF + grp * GP * 128
                    S.op("pool", lambda e, wt=wt, half=half, c0=c0: e.dma_start(
                        out=wt[:, :, half, :], in_=wup[:, :, c0:c0 + GP * 128], max_dma_last_dim=2048),
                        writes=[("wu", slot, half)], dma="wu%d_%d" % (slot, half))
                for j in range(GP):
                    fc = grp * GP + j
                    pb = (fc % 2) * 2
                    for half in range(2):
                        def mm(e, wt=wt, half=half, j=j, pb=pb):
                            for k in range(NCH):
                                ins = e.matmul(pup[:, pb + half, :], wt[:, k, half, j * 128:(j + 1) * 128],
                                               hnT[:, k, :], start=(k == 0), stop=(k == NCH - 1))
                            return ins
                        S.op("pe", mm, reads=[("wu", slot, half)] + hn_keys, writes=[("bank", pb + half)])
                    for half in range(2):
                        ch = half * FCH + fc
                        u = ub[half]
                        a = acc[half]
                        pt = pup[:, pb + half, :]
                        S.op("act", lambda e, u=u, pt=pt: e.copy(out=u[:, 2:NT + 2], in_=pt),
                             reads=[("bank", pb + half)], writes=[("ub", half)])
                        S.op("dve", lambda e, u=u, ch=ch: e.tensor_copy(out=u[:, 0:2], in_=halo[:, ch, :]),
                             reads=["halo", ("halo", ch)], writes=[("ubh", half)])
                        S.op("act", lambda e, a=a, pt=pt, ch=ch: e.activation(
                            out=a[:], in_=pt, func=AF.Identity, scale=cw[:, ch, 2:3], bias=cb[:, ch:ch + 1]),
                            reads=[("bank", pb + half), "pk"], writes=[("acc", half)])
                        S.op("dve", lambda e, u=u, a=a, ch=ch: e.scalar_tensor_tensor(
                            out=a[:], in0=u[:, 1:NT + 1], scalar=cw[:, ch, 1:2], in1=a[:],
                            op0=ALU.mult, op1=ALU.add),
                            reads=[("ub", half), ("ubh", half), ("acc", half), "pk"], writes=[("acc", half)])
                        S.op("dve", lambda e, u=u, a=a, ch=ch: e.scalar_tensor_tensor(
                            out=a[:], in0=u[:, 0:NT], scalar=cw[:, ch, 0:1], in1=a[:],
                            op0=ALU.mult, op1=ALU.add),
                            reads=[("ub", half), ("ubh", half), ("acc", half), "pk"], writes=[("acc", half)])
                        S.op("dve", lambda e, u=u, ch=ch: e.tensor_copy(out=halo[:, ch, :], in_=u[:, NT:NT + 2]),
                             reads=[("ub", half)], writes=[("halo", ch)])
                    S.op("act", lambda e: e.activation(out=sg[:], in_=acc[0][:], func=AF.Silu),
                         reads=[("acc", 0)], writes=["sg"])
                    S.op("dve", lambda e, fc=fc: e.tensor_tensor(out=gbuf[:, fc, :], in0=sg[:], in1=acc[1][:],
                                                                 op=ALU.mult),
                         reads=["sg", ("acc", 1)], writes=[("g", fc)])
            def nxt(dc, tt=tt):
                if tt + 1 < ntile and dc < 3:
                    _pre_norm_tile(kb, S, A, P, src, tt + 1, gT, ident, hnT, xs, hnb, junk, stat, epsc,
                                   subs=([0, 1], [2], [3])[dc])
            _out_proj_tile(kb, S, P, lambda kk, s: gbuf[:, kk, s * 128:(s + 1) * 128], FCH, wdn, wd, "wd", 11,
                           ybuf, lambda pc, s: [("g", pc * 11 + k) for k in range(11)], after_dc=nxt)
            for s in range(nsub):
                _post_norm_residual(kb, S, ybuf, s, src, dst, tt * NT + s * 128, grow, xs, junk, stat, ("ybuf", s), epsc)
        S.emit()


def build(T, layers, do_ffn=True):
    kb = KB(T, layers)
    nc = kb.nc
    x = kb.din("x", [T, D])
    kb.din("ident", [128, 128], BF16)
    for l, m in enumerate(layers):
        kb.din("pk%d" % l, [128, 512])
        kb.din("rows%d" % l, [3, D])
        if do_ffn:
            kb.din("wup%d" % l, [D, 2 * DFF])
            kb.din("wdn%d" % l, [DFF, D])
        if m is not None:
            kb.din("win%d" % l, [D, WIN_COLS[m]])
            kb.din("wout%d" % l, [D, D])
        if m == 1:
            kb.din("posT", [128, T // 128], mybir.dt.int32)
            kb.din("invf", [128, 8])
        if m in (0, 1, 3) and "swamask" not in kb.inputs:
            kb.din("swamask", [128, 256], BF16)
        if m == 3:
            kb.din("utri", [128, 128])
            kb.din("identf", [128, 128])
    y = nc.dram_tensor("y", [T, D], F32, kind="ExternalOutput").ap()
    xres = nc.dram_tensor("xres", [T, D], F32, kind="Internal").ap()
    with ExitStack() as st:
        S = Sched(nc, st)
        cur = x
        nl = len(layers)
        for l, m in enumerate(layers):
            last = (l == nl - 1)
            if m is not None:
                mdst = y if (last and not do_ffn) else xres
                mixer_phase(kb, S, l, m, cur, mdst)
                cur = mdst
            if do_ffn:
                dst = y if last else xres
                ffn_phase(kb, S, l, cur, dst)
                cur = dst
    return kb


def _colsT(v):
    v = np.asarray(v)
    return np.ascontiguousarray(v.reshape(-1, 128).T)


def pack_layer(l, inp, m, j):
    pk = np.zeros((128, 512), np.float32)
    pk[:, 0:16] = _colsT(inp["mix_pre_g"][l])
    pk[:, 16:32] = _colsT(inp["ffn_pre_g"][l])
    cw = np.asarray(inp["ffn_conv_w"][l])
    cwT = np.stack([_colsT(cw[k]) for k in range(3)], axis=2)
    pk[:, 32:32 + 2 * FCH * 3] = cwT.reshape(128, -1)
    pk[:, 296:296 + 2 * FCH] = _colsT(inp["ffn_conv_b"][l])
    if m == 2:
        scw = np.asarray(inp["sc_conv_w"][j])
        pk[:, 384:384 + NCH * 3] = np.stack([_colsT(scw[k]) for k in range(3)], axis=2).reshape(128, -1)
    if m == 1:
        sk = np.asarray(inp["swa_sinks"][j])
        pk[:, 384:400] = np.concatenate([np.tile(sk[0::2][None, :], (64, 1)), np.tile(sk[1::2][None, :], (64, 1))], axis=0)
    if m == 0:
        lbp = np.asarray(inp["hgrn_lb_param"])
        pk[:, 384:464] = np.stack([_colsT(lbp[i]) for i in range(5)], axis=2).reshape(128, -1)
        pk[:, 464] = np.asarray(inp["hgrn_norm_g"][j])
    r3 = np.zeros((D,), np.float32)
    if m == 3:
        r3[0:32] = np.asarray(inp["fox_b_f"][j])
    rows = np.stack([np.asarray(inp["mix_post_g"][l]), np.asarray(inp["ffn_post_g"][l]), r3]).astype(np.float32)
    return pk, rows


class _Common:
    def __init__(self, kb, S, st, l, wo_k):
        nc = kb.nc
        self.st = st

        def A(nm, shape, dt):
            return st.enter_context(nc.sbuf_tensor(kb.name(nm), shape, dt))
        self.A = A
        nsub = NT // 128
        self.xs = [A("xs", [128, D], F32) for _ in range(2)]
        self.hnb = [A("hnb", [128, D], BF16) for _ in range(2)]
        self.junk = A("junk", [128, D], BF16)
        self.stat = A("stat", [128, 16], F32)
        self.hnT = A("hnT", [128, NCH, NT], BF16)
        self.ybuf = A("ybuf", [128, nsub, D], F32)
        self.grow = A("grow", [128, D], F32)
        self.pk = A("pk", [128, 512], F32)
        self.ident = A("ident", [128, 128], BF16)
        self.epsc = A("epsc", [128, 1], F32)
        self.wo = [A("wo", [128, wo_k, 512], BF16) for _ in range(2)]
        self.pA = st.enter_context(nc.psum_tensor(kb.name("pA"), [128, 4, 512], F32))
        self.pB = st.enter_context(nc.psum_tensor(kb.name("pB"), [128, 4, 512], F32))
        pB = self.pB
        pA = self.pA
        self.P = {"tr": [pA[:, 0, :].bitcast(BF16), pA[:, 1, :].bitcast(BF16)], "trbank": [0, 1],
                  "tr45": [pB[:, 0, :].bitcast(BF16), pB[:, 1, :].bitcast(BF16)],
                  "dn": [pB[:, s, :] for s in range(4)], "dnbank": [4, 5, 6, 7]}
        self.gT = self.pk[:, 0:16]
        pk, grow, ident, epsc = self.pk, self.grow, self.ident, self.epsc
        S.op("sp", lambda e: e.dma_start(out=pk[:], in_=kb.inputs["pk%d" % l]), writes=["gT", "pk"], dma="pk")
        S.op("sp", lambda e: e.dma_start(out=grow[:], in_=kb.inputs["rows%d" % l][0].partition_broadcast(128)),
             writes=["grow"], dma="grow")
        S.op("sp", lambda e: e.dma_start(out=ident[:], in_=kb.inputs["ident"]), writes=["ident"], dma="ident")
        S.op("dve", lambda e: e.memset(epsc[:], EPS), writes=["epsc"])

    def bank(self, b):
        return self.pA[:, b, :] if b < 4 else self.pB[:, b - 4, :]

    def pre_norm(self, kb, S, src, tt, subs=None):
        if tt == 0 or subs is not None or not self.pipelined:
            _pre_norm_tile(kb, S, self.A, self.P, src, tt, self.gT, self.ident, self.hnT, self.xs, self.hnb,
                           self.junk, self.stat, self.epsc, subs=subs)
        return [("hnT", s, g) for s in range(NT // 128) for g in range(2)]

    pipelined = False

    def out_proj(self, kb, S, src, dst, tt, oT, wsrc, kpiece, okeys, pipe_src=None):
        nxt = None
        if pipe_src is not None and (tt + 1) * NT < kb.T:
            self.pipelined = True

            def nxt(dc):
                if dc < 3:
                    self.pre_norm(kb, S, pipe_src, tt + 1, subs=([0, 1], [2], [3])[dc])
        _out_proj_tile(kb, S, self.P, lambda kk, s: oT[:, kk, s * 128:(s + 1) * 128], NCH, wsrc, self.wo, "wo",
                       kpiece, self.ybuf, okeys, after_dc=nxt)
        for s in range(NT // 128):
            _post_norm_residual(kb, S, self.ybuf, s, src, dst, tt * NT + s * 128, self.grow, self.xs,
                                self.junk, self.stat, ("ybuf", s), self.epsc)


def sconv_phase(kb, S, l, src, dst):
    nc = kb.nc
    ntile = kb.T // NT
    win = kb.inputs["win%d" % l].rearrange("(c p) f -> p c f", p=128)
    wout = kb.inputs["wout%d" % l].rearrange("(c p) f -> p c f", p=128)
    GP = 2
    with ExitStack() as st:
        C = _Common(kb, S, st, l, 8)
        A = C.A
        wi = [A("wi", [128, NCH, 3, GP * 128], BF16) for _ in range(2)]
        mT = A("mT", [128, NCH, NT], BF16)
        zc = A("zc", [128, NT], F32)
        ub = A("ub", [128, NT + 2], F32)
        acc = A("acc", [128, NT], F32)
        halo = A("halo", [128, NCH, 2], F32)
        cw = C.pk[:, 384:384 + NCH * 3].rearrange("p (c k) -> p c k", k=3)
        S.op("dve", lambda e: e.memset(halo[:], 0.0), writes=["halo"])
        kb.wcnt = {"wi": 0, "wo": 0}
        banksets = [(0, 1, 2), (3, 6, 7)]
        for tt in range(ntile):
            hn_keys = C.pre_norm(kb, S, src, tt)
            for grp in range(NCH // GP):
                slot = kb.wcnt["wi"] % 2
                kb.wcnt["wi"] += 1
                wt = wi[slot]
                for third in range(3):
                    c0 = third * D + grp * GP * 128
                    S.op("pool", lambda e, wt=wt, third=third, c0=c0: e.dma_start(
                        out=wt[:, :, third, :], in_=win[:, :, c0:c0 + GP * 128], max_dma_last_dim=8192),
                        writes=[("wi", slot, third)], dma="wi%d_%d" % (slot, third))
                for j in range(GP):
                    ch = grp * GP + j
                    bs = banksets[ch % 2]
                    for third in range(3):
                        def mm(e, wt=wt, third=third, j=j, bs=bs):
                            for k in range(NCH):
                                ins = e.matmul(C.bank(bs[third]), wt[:, k, third, j * 128:(j + 1) * 128],
                                               C.hnT[:, k, :], start=(k == 0), stop=(k == NCH - 1))
                            return ins
                        S.op("pe", mm, reads=[("wi", slot, third)] + hn_keys, writes=[("bank", bs[third])])
                    pb_, pc_, px_ = C.bank(bs[0]), C.bank(bs[1]), C.bank(bs[2])
                    S.op("act", lambda e, pc_=pc_: e.copy(out=zc[:], in_=pc_), reads=[("bank", bs[1])], writes=["zc"])
                    S.op("dve", lambda e, px_=px_: e.tensor_tensor(out=ub[:, 2:NT + 2], in0=zc[:], in1=px_, op=ALU.mult),
                         reads=["zc", ("bank", bs[2])], writes=["ub"])
                    S.op("dve", lambda e, ch=ch: e.tensor_copy(out=ub[:, 0:2], in_=halo[:, ch, :]),
                         reads=["halo", ("halo", ch)], writes=["ubh"])
                    S.op("act", lambda e, ch=ch: e.activation(out=acc[:], in_=ub[:, 2:NT + 2], func=AF.Identity,
                                                              scale=cw[:, ch, 2:3]),
                         reads=["ub", "pk"], writes=["acc"])
                    S.op("dve", lambda e, ch=ch: e.scalar_tensor_tensor(
                        out=acc[:], in0=ub[:, 1:NT + 1], scalar=cw[:, ch, 1:2], in1=acc[:], op0=ALU.mult, op1=ALU.add),
                        reads=["ub", "ubh", "acc", "pk"], writes=["acc"])
                    S.op("dve", lambda e, ch=ch: e.scalar_tensor_tensor(
                        out=acc[:], in0=ub[:, 0:NT], scalar=cw[:, ch, 0:1], in1=acc[:], op0=ALU.mult, op1=ALU.add),
                        reads=["ub", "ubh", "acc", "pk"], writes=["acc"])
                    S.op("dve", lambda e, ch=ch: e.tensor_copy(out=halo[:, ch, :], in_=ub[:, NT:NT + 2]),
                         reads=["ub"], writes=[("halo", ch)])
                    S.op("dve", lambda e, ch=ch, pb_=pb_: e.tensor_tensor(out=mT[:, ch, :], in0=acc[:], in1=pb_, op=ALU.mult),
                         reads=["acc", ("bank", bs[0])], writes=[("oT", ch)])
            C.out_proj(kb, S, src, dst, tt, mT, wout, 8, lambda pc, s: [("oT", pc * 8 + k) for k in range(8)], pipe_src=src)
        S.emit()


import math
MAGIC = 12582912.0
PIS = 3.1415925


def swa_phase(kb, S, l, src, dst):
    nc = kb.nc
    T = kb.T
    ntile = T // NT
    nblk = T // 128
    I32 = mybir.dt.int32
    win = kb.inputs["win%d" % l].rearrange("(c p) f -> p c f", p=128)
    wout = kb.inputs["wout%d" % l].rearrange("(c p) f -> p c f", p=128)
    with ExitStack() as st:
        C = _Common(kb, S, st, l, 8)
        A = C.A
        wi = [A("wi", [128, NCH, 512], BF16) for _ in range(2)]
        qr = A("qr", [128, D], BF16)
        kr = A("kr", [128, 512], BF16)
        qT = A("qT", [128, NCH, NT], BF16)
        kT2 = A("kT2", [128, 4, NT + 128], BF16)
        VAB = A("VAB", [128, 5, 4, 2, 128], BF16)
        DAB = A("DAB", [128, 2, 128], BF16)
        mask = A("mask", [128, 256], BF16)
        oT = A("oT", [128, NCH, NT], BF16)
        PT = [A("PT", [128, 2, 256], BF16) for _ in range(2)]
        tmpA = A("tmpA", [128, 8, 16], F32)
        tmpB = A("tmpB", [128, 8, 16], F32)
        dens = A("dens", [128, NT], F32)
        esink = A("esink", [128, 16], F32)
        posi = A("posi", [128, nblk], I32)
        posf = A("posf", [128, nblk], F32)
        invf = A("invf", [128, 8], F32)
        ang = A("ang", [128, nblk, 8], F32)
        rs = A("rs", [128, nblk, 8], F32)
        rc = A("rc", [128, nblk, 8], F32)
        cs2 = A("cs2", [128, nblk, 16], F32)
        sn = A("sn", [128, nblk, 16], F32)
        S.op("sp", lambda e: e.dma_start(out=posi[:], in_=kb.inputs["posT"]), writes=["posi"], dma="posi")
        S.op("sp", lambda e: e.dma_start(out=invf[:], in_=kb.inputs["invf"]), writes=["invf"], dma="invf")
        S.op("sp", lambda e: e.dma_start(out=mask[:], in_=kb.inputs["swamask"]), writes=["mask"], dma="mask")
        S.op("dve", lambda e: e.memset(VAB[:], 0.0), writes=["VAB"])
        S.op("dve", lambda e: e.memset(DAB[:], 0.0), writes=["DAB"])
        S.op("dve", lambda e: e.memset(DAB[:, 0, 0:64], 1.0), reads=["DAB"], writes=["DAB"])
        S.op("dve", lambda e: e.memset(DAB[:, 1, 64:128], 1.0), reads=["DAB"], writes=["DAB"])
        S.op("act", lambda e: e.activation(out=esink[:], in_=C.pk[:, 384:400], func=AF.Exp), reads=["pk"], writes=["esink"])
        S.op("dve", lambda e: e.tensor_copy(out=posf[:], in_=posi[:]), reads=["posi"], writes=["posf"])
        S.op("dve", lambda e: e.tensor_tensor(out=ang[:], in0=posf[:].unsqueeze(2).broadcast_to([128, nblk, 8]),
                                              in1=invf[:].unsqueeze(1).broadcast_to([128, nblk, 8]), op=ALU.mult),
             reads=["posf", "invf"], writes=["ang"])

        def rr(dstt, shift, key):
            S.op("dve", lambda e: e.tensor_scalar(out=dstt[:], in0=ang[:], scalar1=shift, scalar2=1.0 / (2 * math.pi),
                                                  op0=ALU.add, op1=ALU.mult), reads=["ang"], writes=[key])
            S.op("dve", lambda e: e.tensor_scalar(out=dstt[:], in0=dstt[:], scalar1=MAGIC, scalar2=MAGIC,
                                                  op0=ALU.add, op1=ALU.subtract), reads=[key], writes=[key])
            S.op("dve", lambda e: e.scalar_tensor_tensor(out=dstt[:], in0=dstt[:], scalar=-2 * math.pi, in1=ang[:],
                                                         op0=ALU.mult, op1=ALU.add), reads=[key, "ang"], writes=[key])
            S.op("dve", lambda e: e.tensor_scalar(out=dstt[:], in0=dstt[:], scalar1=shift, scalar2=None, op0=ALU.add),
                 reads=[key], writes=[key])
            S.op("dve", lambda e: e.tensor_scalar(out=dstt[:], in0=dstt[:], scalar1=-PIS, scalar2=PIS,
                                                  op0=ALU.max, op1=ALU.min), reads=[key], writes=[key])
        rr(rs, 0.0, "rs")
        rr(rc, 0.5 * math.pi, "rc")
        S.op("act", lambda e: e.activation(out=sn[:, :, 8:16], in_=rs[:], func=AF.Sin), reads=["rs"], writes=["sn1"])
        S.op("act", lambda e: e.activation(out=cs2[:, :, 0:8], in_=rc[:], func=AF.Sin), reads=["rc"], writes=["cs1"])
        S.op("dve", lambda e: e.tensor_copy(out=cs2[:, :, 8:16], in_=cs2[:, :, 0:8]), reads=["cs1"], writes=["cs2"])
        S.op("dve", lambda e: e.tensor_scalar(out=sn[:, :, 0:8], in0=sn[:, :, 8:16], scalar1=-1.0, scalar2=None,
                                              op0=ALU.mult), reads=["sn1"], writes=["sn0"])
        tab_keys = ["cs1", "cs2", "sn0", "sn1"]

        def rope_evac(pv, ov, blk, pkey, okey):
            S.op("act", lambda e: e.copy(out=ov[:, :, 16:64], in_=pv[:, :, 16:64]), reads=[pkey], writes=[okey])
            S.op("dve", lambda e: e.tensor_tensor(out=tmpA[:], in0=pv[:, :, 0:16],
                                                  in1=cs2[:, blk, :].unsqueeze(1).broadcast_to([128, 8, 16]), op=ALU.mult),
                 reads=[pkey] + tab_keys, writes=["tmpA"])
            S.op("dve", lambda e: e.tensor_tensor(out=tmpB[:, :, 0:8], in0=pv[:, :, 8:16],
                                                  in1=sn[:, blk, 0:8].unsqueeze(1).broadcast_to([128, 8, 8]), op=ALU.mult),
                 reads=[pkey] + tab_keys, writes=["tmpB"])
            S.op("dve", lambda e: e.tensor_tensor(out=tmpB[:, :, 8:16], in0=pv[:, :, 0:8],
                                                  in1=sn[:, blk, 8:16].unsqueeze(1).broadcast_to([128, 8, 8]), op=ALU.mult),
                 reads=[pkey] + tab_keys, writes=["tmpB"])
            S.op("dve", lambda e: e.tensor_tensor(out=ov[:, :, 0:16], in0=tmpA[:], in1=tmpB[:], op=ALU.add),
                 reads=["tmpA", "tmpB"], writes=[okey])

        kb.wcnt = {"wi": 0, "wo": 0}
        qrs = [A("qrs", [128, 512], BF16) for _ in range(2)]
        nq_ev = 0
        for tt in range(ntile):
            hn_keys = C.pre_norm(kb, S, src, tt)
            first = (tt == 0)
            import os
            DBG = int(os.environ.get("KDBG", "0"))
            if DBG == 3:
                continue
            for grp in range(6):
                if DBG == 4 and grp < 5:
                    continue
                if DBG == 5 and grp != 0:
                    continue
                slot = kb.wcnt["wi"] % 2
                kb.wcnt["wi"] += 1
                wt = wi[slot]
                ncol = 256 if grp == 5 else 512
                c0 = grp * 512
                S.op("pool", lambda e, wt=wt, c0=c0, ncol=ncol: e.dma_start(
                    out=wt[:, :, 0:ncol], in_=win[:, :, c0:c0 + ncol], max_dma_last_dim=8192),
                    writes=[("wi", slot)], dma="wi%d" % slot)
                for s in range(4):
                    blk = tt * 4 + s
                    b = (grp * 4 + s) % 2
                    pbank = C.bank(b)

                    def mm(e, wt=wt, s=s, ncol=ncol, pbank=pbank):
                        for k in range(NCH):
                            ins = e.matmul(pbank[:, 0:ncol], C.hnT[:, k, s * 128:(s + 1) * 128], wt[:, k, 0:ncol],
                                           start=(k == 0), stop=(k == NCH - 1))
                        return ins
                    S.op("pe", mm, reads=[("wi", slot)] + hn_keys, writes=[("bank", b)])
                    if grp < 5:
                        qs = nq_ev % 2
                        nq_ev += 1
                        stg = qrs[qs]
                        rope_evac(pbank.rearrange("p (h d) -> p h d", d=64), stg[:].rearrange("p (h d) -> p h d", d=64),
                                  blk, ("bank", b), ("qrs", qs))
                        tb = qs
                        pt = C.P["tr45"][tb]

                        def trq(e, pt=pt, stg=stg):
                            for j in range(4):
                                ins = e.transpose(pt[:, j * 128:(j + 1) * 128], stg[:, j * 128:(j + 1) * 128], C.ident[:])
                            return ins
                        S.op("pe", trq, reads=[("qrs", qs), "ident"], writes=[("bank", 4 + tb)])
                        pview = pt[:, 0:512].rearrange("p (g t) -> p g t", g=4)
                        if grp < 4:
                            S.op("act", lambda e, s=s, grp=grp, pview=pview: e.copy(
                                out=qT[:, grp * 4:(grp + 1) * 4, s * 128:(s + 1) * 128], in_=pview),
                                reads=[("bank", 4 + tb)], writes=[("qT", s, grp)])
                        else:
                            S.op("act", lambda e, s=s, pview=pview: e.copy(
                                out=kT2[:, :, (s + 1) * 128:(s + 2) * 128], in_=pview),
                                reads=[("bank", 4 + tb)], writes=[("kT2", s + 1)])
                    else:
                        pv = pbank[:, 0:256].rearrange("p (g d) -> p g d", d=64)
                        S.op("act", lambda e, s=s, pv=pv: e.copy(out=VAB[:, s + 1, :, 0, 0:64], in_=pv),
                             reads=[("bank", b), "VAB"], writes=[("VABa", s + 1)])
                        S.op("act", lambda e, s=s, pv=pv: e.copy(out=VAB[:, s + 1, :, 1, 64:128], in_=pv),
                             reads=[("bank", b), "VAB"], writes=[("VABb", s + 1)])
            if DBG in (1, 4, 5):
                continue
            asteps = [(c, kbk) for c in range(NCH) for kbk in range(0 if first else -1, 4)]
            kmin = 0 if first else -1

            def a_geom(i):
                c, kbk = asteps[i]
                g = c // 4
                nb, db = ((6, 7), (0, 1))[c % 2]
                slot = kbk + 1
                qlo = max(kbk, 0) * 128
                qhi = min(kbk + 2, 4) * 128
                ps = i % 2
                sb = 2 + 2 * ps
                stp = C.pA[:, 2:4, :] if ps == 0 else C.pB[:, 0:2, :]
                return c, kbk, g, nb, db, slot, qlo, qhi, qhi - qlo, ps, sb, stp

            def a_st(i):
                c, kbk, g, nb, db, slot, qlo, qhi, nq, ps, sb, stp = a_geom(i)
                qkeys = [("qT", s, c // 4) for s in range(4)]

                def mst(e):
                    for h in range(2):
                        ins = e.matmul(stp[:, h, 0:nq], kT2[h * 64:(h + 1) * 64, g, slot * 128:(slot + 1) * 128],
                                       qT[h * 64:(h + 1) * 64, c, qlo:qhi], start=True, stop=True)
                    return ins
                S.op("pe", mst, reads=[("kT2", slot)] + qkeys, writes=[("bank", sb), ("bank", sb + 1)])

            def a_rest(i, first=first):
                c, kbk, g, nb, db, slot, qlo, qhi, nq, ps, sb, stp = a_geom(i)
                num, den = C.bank(nb), C.bank(db)
                m0 = 128 if kbk == -1 else 0
                ptile = PT[ps]
                S.op("act", lambda e: e.activation(out=ptile[:, :, 0:nq], in_=stp[:, :, 0:nq], func=AF.Exp, scale=0.125),
                     reads=[("bank", sb), ("bank", sb + 1)], writes=[("PT", ps)])
                S.op("dve", lambda e: e.tensor_tensor(
                    out=ptile[:, :, 0:nq], in0=ptile[:, :, 0:nq],
                    in1=mask[:, m0:m0 + nq].unsqueeze(1).broadcast_to([128, 2, nq]), op=ALU.mult),
                    reads=[("PT", ps), "mask"], writes=[("PT", ps)])

                def mpv(e):
                    for j in range(nq // 128):
                        qb = qlo // 128 + j
                        stt = (kbk == qb - 1) or (kbk == qb and qb == 0 and first)
                        spp = (kbk == qb)
                        cs = slice(qb * 128, (qb + 1) * 128)
                        ps_ = slice(j * 128, (j + 1) * 128)
                        e.matmul(num[:, cs], VAB[:, slot, g, 0, :], ptile[:, 0, ps_], start=stt, stop=False)
                        e.matmul(num[:, cs], VAB[:, slot, g, 1, :], ptile[:, 1, ps_], start=False, stop=spp)
                        e.matmul(den[:, cs], DAB[:, 0, :], ptile[:, 0, ps_], start=stt, stop=False)
                        ins = e.matmul(den[:, cs], DAB[:, 1, :], ptile[:, 1, ps_], start=False, stop=spp)
                    return ins
                S.op("pe", mpv, reads=[("PT", ps), ("VABa", slot), ("VABb", slot), "VAB", "DAB"],
                     writes=[("bank", nb), ("bank", db)])
                if kbk == 3:
                    def fin():
                        S.op("dve", lambda e: e.tensor_scalar(out=dens[:], in0=den, scalar1=esink[:, c:c + 1],
                                                              scalar2=None, op0=ALU.add),
                             reads=[("bank", db), "esink"], writes=["dens"])
                        S.op("dve", lambda e: e.reciprocal(out=dens[:], in_=dens[:]), reads=["dens"], writes=["dens"])
                        S.op("dve", lambda e: e.tensor_tensor(out=oT[:, c, :], in0=num, in1=dens[:], op=ALU.mult),
                             reads=[("bank", nb), "dens"], writes=[("oT", c)])
                    apend.append((i + 2, fin))

            apend = []
            ALOOK = 1
            for i in range(len(asteps) + ALOOK):
                if i < len(asteps):
                    a_st(i)
                k = i - ALOOK
                if k >= 0:
                    a_rest(k)
                    while apend and (apend[0][0] <= k or k + 1 == len(asteps)):
                        apend.pop(0)[1]()
            if DBG == 2:
                continue
            if tt < ntile - 1:
                S.op("act", lambda e: e.copy(out=kT2[:, :, 0:128], in_=kT2[:, :, NT:NT + 128]),
                     reads=[("kT2", 4)], writes=[("kT2", 0)])
                S.op("act", lambda e: e.copy(out=VAB[:, 0], in_=VAB[:, 4]),
                     reads=[("VABa", 4), ("VABb", 4), "VAB"], writes=[("VABa", 0), ("VABb", 0)])
            C.out_proj(kb, S, src, dst, tt, oT, wout, 8, lambda pc, s: [("oT", pc * 8 + k) for k in range(8)], pipe_src=src)
        S.emit()


def fox_phase(kb, S, l, src, dst):
    nc = kb.nc
    T = kb.T
    ntile = T // NT
    nblk = T // 128
    NH = 32
    KA = 67
    win = kb.inputs["win%d" % l].rearrange("(c p) f -> p c f", p=128)
    wout = kb.inputs["wout%d" % l].rearrange("(c p) f -> p c f", p=128)

    def scr(nm, shape, dt):
        return nc.dram_tensor(kb.name(nm), list(shape), dt, kind="Internal").ap()
    QT = scr("QT", [NH, KA, T], BF16)
    KT = scr("KT", [NH, KA, T], BF16)
    GT = scr("GT", [NCH, 128, T], F32)
    VS = scr("VS", [T, D], BF16)
    OT = scr("OT", [NCH, 128, T], BF16)
    with ExitStack() as st0:
        cs_all = st0.enter_context(nc.sbuf_tensor(kb.name("cs_all"), [128, nblk, 32], F32))
        with ExitStack() as st:
            def A(nm, shape, dt):
                return st.enter_context(nc.sbuf_tensor(kb.name(nm), shape, dt))
            xs = [A("xs", [128, D], F32) for _ in range(2)]
            hnb = [A("hnb", [128, D], BF16) for _ in range(2)]
            junk = A("junk", [128, D], BF16)
            stat = A("stat", [128, 16], F32)
            hnT = A("hnT", [128, NCH, NT], BF16)
            pk = A("pk", [128, 512], F32)
            ident = A("ident", [128, 128], BF16)
            identf = A("identf", [128, 128], F32)
            epsc = A("epsc", [128, 1], F32)
            onec = A("onec", [128, 1], F32)
            GP = 2
            wi = [A("wi", [128, NCH, 3, GP * 128], BF16) for _ in range(2)]
            wv = [A("wv", [128, NCH, 512], BF16) for _ in range(2)]
            wf = A("wf", [128, NCH, 32], BF16)
            qst = [A("qst", [128, GP, NT], BF16) for _ in range(2)]
            kst = [A("kst", [128, GP, NT], BF16) for _ in range(2)]
            gst = [A("gst", [128, GP, NT], F32) for _ in range(2)]
            vst = A("vst", [128, 4, D], BF16)
            bfb = A("bfb", [128, 32], F32)
            zf = A("zf", [128, 32], F32)
            spf = A("spf", [128, 32], F32)
            cbase = A("cbase", [128, 32], F32)
            Utri = A("Utri", [128, 128], F32)
            ones = A("ones", [128, 128], F32)
            a32 = A("a32", [32, NT], F32)
            r32 = A("r32", [32, NT], F32)
            aq = A("aq", [32, 3, NT], BF16)
            ak = A("ak", [32, 3, NT], BF16)
            pA = st.enter_context(nc.psum_tensor(kb.name("pA"), [128, 4, 512], F32))
            pB = st.enter_context(nc.psum_tensor(kb.name("pB"), [128, 4, 512], F32))
            P = {"tr": [pB[:, 0, :].bitcast(BF16), pB[:, 1, :].bitcast(BF16)], "trbank": [4, 5]}
            gT = pk[:, 0:16]
            S.op("sp", lambda e: e.dma_start(out=pk[:], in_=kb.inputs["pk%d" % l]), writes=["gT", "pk"], dma="pk")
            S.op("sp", lambda e: e.dma_start(out=ident[:], in_=kb.inputs["ident"]), writes=["ident"], dma="ident")
            S.op("sp", lambda e: e.dma_start(out=identf[:], in_=kb.inputs["identf"]), writes=["identf"], dma="identf")
            S.op("sp", lambda e: e.dma_start(out=bfb[:], in_=kb.inputs["rows%d" % l][2, 0:32].partition_broadcast(128)),
                 writes=["bfb"], dma="bfb")
            S.op("sp", lambda e: e.dma_start(out=Utri[:], in_=kb.inputs["utri"]), writes=["utri"], dma="utri")
            S.op("pool", lambda e: e.dma_start(out=wf[:], in_=win[:, :, 6144:6176]), writes=["wf"], dma="wf")
            S.op("dve", lambda e: e.memset(epsc[:], EPS), writes=["epsc"])
            S.op("dve", lambda e: e.memset(onec[:], 1.0), writes=["onec"])
            S.op("dve", lambda e: e.memset(ones[:], 1.0), writes=["ones"])
            S.op("dve", lambda e: e.memset(cbase[:], 0.0), writes=["cbase"])
            S.op("dve", lambda e: e.memset(ak[:], 1.0), writes=["ak"])
            nwi = 0
            nwv = 0
            nst = 0
            for tt in range(ntile):
                _pre_norm_tile(kb, S, A, P, src, tt, gT, ident, hnT, xs, hnb, junk, stat, epsc)
                hn_keys = [("hnT", s, g) for s in range(4) for g in range(2)]
                c0t = tt * NT
                for grp in range(NCH // GP):
                    slot = nwi % 2
                    nwi += 1
                    wt = wi[slot]
                    for third, cbase_col in enumerate((0, 2048, 6176)):
                        c0 = cbase_col + grp * GP * 128
                        S.op("pool", lambda e, wt=wt, third=third, c0=c0: e.dma_start(
                            out=wt[:, :, third, :], in_=win[:, :, c0:c0 + GP * 128], max_dma_last_dim=8192),
                            writes=[("wi", slot, third)], dma="wi%d_%d" % (slot, third))
                    ss = nst % 2
                    nst += 1
                    for j in range(GP):
                        for third in range(3):
                            b = (j * 3 + third) % 4

                            def mm(e, wt=wt, third=third, j=j, b=b):
                                for k in range(NCH):
                                    ins = e.matmul(pA[:, b, :], wt[:, k, third, j * 128:(j + 1) * 128], hnT[:, k, :],
                                                   start=(k == 0), stop=(k == NCH - 1))
                                return ins
                            S.op("pe", mm, reads=[("wi", slot, third)] + hn_keys, writes=[("bank", b)])
                            if third == 0:
                                S.op("act", lambda e, b=b, ss=ss, j=j: e.copy(out=qst[ss][:, j, :], in_=pA[:, b, :]),
                                     reads=[("bank", b)], writes=[("qst", ss, j)])
                            elif third == 1:
                                S.op("dve", lambda e, b=b, ss=ss, j=j: e.tensor_copy(out=kst[ss][:, j, :], in_=pA[:, b, :]),
                                     reads=[("bank", b)], writes=[("kst", ss, j)])
                            else:
                                S.op("act", lambda e, b=b, ss=ss, j=j: e.activation(out=gst[ss][:, j, :], in_=pA[:, b, :],
                                                                                    func=AF.Sigmoid),
                                     reads=[("bank", b)], writes=[("gst", ss, j)])
                    cc = grp * GP
                    for j in range(GP):
                        for hh in range(2):
                            h = 2 * (cc + j) + hh
                            S.op("sp", lambda e, ss=ss, j=j, hh=hh, h=h, c0t=c0t: e.dma_start(
                                out=QT[h, 0:64, c0t:c0t + NT], in_=qst[ss][hh * 64:(hh + 1) * 64, j, :]),
                                reads=[("qst", ss, j)], dma="qst%d_%d" % (ss, j))
                            S.op("sp", lambda e, ss=ss, j=j, hh=hh, h=h, c0t=c0t: e.dma_start(
                                out=KT[h, 0:64, c0t:c0t + NT], in_=kst[ss][hh * 64:(hh + 1) * 64, j, :]),
                                reads=[("kst", ss, j)], dma="kst%d_%d" % (ss, j))
                    S.op("sp", lambda e, ss=ss, cc=cc, c0t=c0t: e.dma_start(
                        out=GT[cc:cc + GP, :, c0t:c0t + NT].rearrange("c p t -> p c t"), in_=gst[ss][:]),
                        reads=[("gst", ss, j) for j in range(GP)], dma="gst%d" % ss)
                for grp in range(4):
                    slot = nwv % 2
                    nwv += 1
                    wt = wv[slot]
                    c0 = 4096 + grp * 512
                    S.op("pool", lambda e, wt=wt, c0=c0: e.dma_start(out=wt[:], in_=win[:, :, c0:c0 + 512],
                                                                     max_dma_last_dim=8192),
                         writes=[("wv", slot)], dma="wv%d" % slot)
                    for s in range(4):
                        b = s % 2

                        def mmv(e, wt=wt, s=s, b=b):
                            for k in range(NCH):
                                ins = e.matmul(pA[:, b, :], hnT[:, k, s * 128:(s + 1) * 128], wt[:, k, :],
                                               start=(k == 0), stop=(k == NCH - 1))
                            return ins
                        S.op("pe", mmv, reads=[("wv", slot)] + hn_keys, writes=[("bank", b)])
                        S.op("act", lambda e, s=s, b=b, grp=grp: e.copy(out=vst[:, s, grp * 512:(grp + 1) * 512],
                                                                        in_=pA[:, b, :]),
                             reads=[("bank", b)], writes=[("vst", s)])
                for s in range(4):
                    r0 = c0t + s * 128
                    S.op("sp", lambda e, s=s, r0=r0: e.dma_start(out=VS[r0:r0 + 128, :], in_=vst[:, s, :]),
                         reads=[("vst", s)], dma="vst%d" % s)
                for s in range(4):
                    blk = tt * 4 + s

                    def mmf(e, s=s):
                        for k in range(NCH):
                            ins = e.matmul(pA[:, 2, 0:32], hnT[:, k, s * 128:(s + 1) * 128], wf[:, k, :],
                                           start=(k == 0), stop=(k == NCH - 1))
                        return ins
                    S.op("pe", mmf, reads=["wf"] + hn_keys, writes=[("bank", 2)])
                    S.op("dve", lambda e: e.tensor_tensor(out=zf[:], in0=pA[:, 2, 0:32], in1=bfb[:], op=ALU.add),
                         reads=[("bank", 2), "bfb"], writes=["zf"])
                    S.op("act", lambda e: e.activation(out=zf[:], in_=zf[:], func=AF.Exp, scale=-1.0),
                         reads=["zf"], writes=["zf"])
                    S.op("act", lambda e: e.activation(out=spf[:], in_=zf[:], func=AF.Ln, bias=onec[:, 0:1]),
                         reads=["zf", "onec"], writes=["spf"])

                    def mmc(e):
                        e.matmul(pA[:, 3, 0:32], Utri[:], spf[:], start=True, stop=True)
                        return e.matmul(pA[:, 3, 32:64], ones[:], spf[:], start=True, stop=True)
                    S.op("pe", mmc, reads=["utri", "ones", "spf"], writes=[("bank", 3)])
                    S.op("dve", lambda e, blk=blk: e.tensor_tensor(out=cs_all[:, blk, :], in0=pA[:, 3, 0:32], in1=cbase[:],
                                                                   op=ALU.add),
                         reads=[("bank", 3), "cbase"], writes=[("cs", blk)])
                    S.op("dve", lambda e: e.tensor_tensor(out=cbase[:], in0=pA[:, 3, 32:64], in1=cbase[:], op=ALU.add),
                         reads=[("bank", 3), "cbase"], writes=["cbase"])
                    S.op("pe", lambda e, blk=blk: e.transpose(pB[0:32, 2, 0:128], cs_all[:, blk, :], identf[:]),
                         reads=[("cs", blk), "identf"], writes=[("bank", 6)])
                    S.op("dve", lambda e, s=s: e.tensor_scalar(out=a32[:, s * 128:(s + 1) * 128], in0=pB[0:32, 2, 0:128],
                                                               scalar1=-8.0, scalar2=None, op0=ALU.mult),
                         reads=[("bank", 6)], writes=[("a32", s)])
                a_keys = [("a32", s) for s in range(4)]
                S.op("dve", lambda e: e.tensor_copy(out=aq[:, 0, :], in_=a32[:]), reads=a_keys, writes=["aq0"])
                S.op("dve", lambda e: e.tensor_tensor(out=r32[:], in0=a32[:], in1=aq[:, 0, :], op=ALU.subtract),
                     reads=a_keys + ["aq0"], writes=["r32"])
                S.op("dve", lambda e: e.tensor_copy(out=aq[:, 1, :], in_=r32[:]), reads=["r32"], writes=["aq1"])
                S.op("dve", lambda e: e.tensor_tensor(out=r32[:], in0=r32[:], in1=aq[:, 1, :], op=ALU.subtract),
                     reads=["r32", "aq1"], writes=["r32"])
                S.op("dve", lambda e: e.tensor_copy(out=aq[:, 2, :], in_=r32[:]), reads=["r32"], writes=["aq2"])
                S.op("sp", lambda e, c0t=c0t: e.dma_start(out=QT[:, 64:67, c0t:c0t + NT], in_=aq[:]),
                     reads=["aq0", "aq1", "aq2"], dma="aq")
                S.op("sp", lambda e, c0t=c0t: e.dma_start(out=KT[:, 64:67, c0t:c0t + NT], in_=ak[:]),
                     reads=["ak"], dma="ak")
            S.emit()
        with ExitStack() as st:
            def A(nm, shape, dt):
                return st.enter_context(nc.sbuf_tensor(kb.name(nm), shape, dt))
            Qh = [A("Qh", [128, T], BF16) for _ in range(2)]
            Kh = [A("Kh", [128, T], BF16) for _ in range(2)]
            Gh = [A("Gh", [64, T], F32) for _ in range(2)]
            Vh = [A("Vh", [128, nblk, 128], BF16) for _ in range(2)]
            od = [A("od", [128, NT], F32) for _ in range(2)]
            dl = [A("dl", [64, NT], F32) for _ in range(2)]
            tril = A("tril", [128, 128], BF16)
            PT = [A("PT", [128, NT], BF16) for _ in range(4)]
            rden = A("rden", [64, NT], F32)
            ofp = A("ofp", [64, NT], F32)
            ost = [A("ost", [64, NT], BF16) for _ in range(2)]
            pA = st.enter_context(nc.psum_tensor(kb.name("pA"), [128, 4, 512], F32))
            pB = st.enter_context(nc.psum_tensor(kb.name("pB"), [128, 4, 512], F32))
            S.op("sp", lambda e: e.dma_start(out=tril[:], in_=kb.inputs["swamask"][:, 0:128]), writes=["tril"], dma="tril")
            for i in range(2):
                S.op("dve", lambda e, i=i: e.memset(Vh[i][:], 1.0), writes=[("Vh1", i)])
                S.op("dve", lambda e, i=i: e.memset(Qh[i][:], 0.0), writes=[("Qh", i)])
                S.op("dve", lambda e, i=i: e.memset(Kh[i][:], 0.0), writes=[("Kh", i)])
            VSv = VS.rearrange("(b p) f -> p b f", p=128)
            NSB = 4
            LOOK = 2
            steps = []
            for h in range(NH):
                for Q in range(ntile):
                    for kbk in range(4 * Q + 4):
                        steps.append((h, Q, kbk))

            def emit_loads(h):
                sl = h % 2
                c, hh = h // 2, h % 2
                S.op("sp", lambda e: e.dma_start(out=Qh[sl][0:KA, :], in_=QT[h]), writes=[("Qh", sl)], dma="Qh%d" % sl)
                S.op("sp", lambda e: e.dma_start(out=Kh[sl][0:KA, :], in_=KT[h]), writes=[("Kh", sl)], dma="Kh%d" % sl)
                S.op("sp", lambda e: e.dma_start(out=Gh[sl][:], in_=GT[c, hh * 64:(hh + 1) * 64, :]),
                     writes=[("Gh", sl)], dma="Gh%d" % sl)
                S.op("sp", lambda e: e.dma_start(out=Vh[sl][:, :, 0:64], in_=VSv[:, :, h * 64:(h + 1) * 64]),
                     reads=[("Vh1", sl)], writes=[("Vh", sl)], dma="Vh%d" % sl)

            def geom(i):
                h, Q, kbk = steps[i]
                j = kbk - 4 * Q
                q0 = max(j, 0) * 128
                return h, Q, kbk, j, q0, NT - q0, i % NSB

            def emit_st(i):
                h, Q, kbk, j, q0, nq, ps = geom(i)
                sl = h % 2
                qa = Q * NT + q0
                stp = pA[:, ps, :]
                S.op("pe", lambda e: e.matmul(stp[:, 0:nq], Kh[sl][:, kbk * 128:(kbk + 1) * 128],
                                              Qh[sl][:, qa:qa + nq], start=True, stop=True),
                     reads=[("Qh", sl), ("Kh", sl)], writes=[("bank", ps)])

            def emit_rest(i):
                h, Q, kbk, j, q0, nq, ps = geom(i)
                sl = h % 2
                c, hh = h // 2, h % 2
                fb = (h * ntile + Q) % 2
                pnd = pB[:, fb, :]
                nbk = 4 + fb
                nkb = 4 * Q + 4
                stp = pA[:, ps, :]
                ptile = PT[ps]
                S.op("act", lambda e: e.activation(out=ptile[:, 0:nq], in_=stp[:, 0:nq], func=AF.Exp, scale=0.125,
                                                   bias=cs_all[:, kbk, h:h + 1]),
                     reads=[("bank", ps)], writes=[("PT", ps)])
                if j >= 0:
                    S.op("dve", lambda e: e.tensor_tensor(out=ptile[:, 0:128], in0=ptile[:, 0:128], in1=tril[:],
                                                          op=ALU.mult),
                         reads=[("PT", ps), "tril"], writes=[("PT", ps)])
                S.op("pe", lambda e: e.matmul(pnd[:, q0:NT], Vh[sl][:, kbk, :], ptile[:, 0:nq],
                                              start=(kbk == 0), stop=(kbk == nkb - 1)),
                     reads=[("PT", ps), ("Vh", sl), ("Vh1", sl)], writes=[("bank", nbk)])
                if kbk == nkb - 1:
                    pending.append((i + 3, lambda: finalize(h, Q, fb, pnd, nbk, sl, c, hh)))

            def finalize(h, Q, fb, pnd, nbk, sl, c, hh):
                if True:
                    os_ = fb
                    S.op("act", lambda e: e.copy(out=od[os_][:], in_=pnd), reads=[("bank", nbk)], writes=[("od", os_)])
                    S.op("sp", lambda e: e.dma_start(out=dl[os_][:], in_=od[os_][64:128, :]),
                         reads=[("od", os_)], writes=[("dl", os_)], dma="dl%d" % os_)
                    S.op("dve", lambda e: e.reciprocal(out=dl[os_][:], in_=dl[os_][:]), reads=[("dl", os_)], writes=[("dl", os_)])
                    S.op("dve", lambda e: e.tensor_tensor(out=ofp[:], in0=od[os_][0:64, :], in1=dl[os_][:], op=ALU.mult),
                         reads=[("od", os_), ("dl", os_)], writes=["ofp"])
                    S.op("dve", lambda e: e.tensor_tensor(out=ost[os_][:], in0=ofp[:], in1=Gh[sl][:, Q * NT:(Q + 1) * NT],
                                                          op=ALU.mult),
                         reads=["ofp", ("Gh", sl)], writes=[("ost", os_)])
                    S.op("sp", lambda e: e.dma_start(out=OT[c, hh * 64:(hh + 1) * 64, Q * NT:(Q + 1) * NT], in_=ost[os_][:]),
                         reads=[("ost", os_)], dma="ost%d" % os_)

            pending = []
            emit_loads(0)
            emit_loads(1)
            for i in range(len(steps) + LOOK):
                if i < len(steps):
                    emit_st(i)
                k = i - LOOK
                if k >= 0:
                    emit_rest(k)
                    hk = steps[k][0]
                    if (k + 1 == len(steps) or steps[k + 1][0] != hk) and hk + 2 < NH:
                        pending.append((k + 3, lambda hk=hk: emit_loads(hk + 2)))
                    while pending and (pending[0][0] <= k or k + 1 == len(steps)):
                        pending.pop(0)[1]()
            S.emit()
    with ExitStack() as st:
        C = _Common(kb, S, st, l, 8)
        oT = C.A("oT", [128, NCH, NT], BF16)
        kb.wcnt = {"wo": 0}
        for tt in range(ntile):
            S.op("sp", lambda e, tt=tt: e.dma_start(out=oT[:], in_=OT[:, :, tt * NT:(tt + 1) * NT].rearrange("c p t -> p c t")),
                 writes=["oT"], dma="oTl")
            C.out_proj(kb, S, src, dst, tt, oT, wout, 8, lambda pc, s: ["oT"])
        S.emit()


def hgrn_phase(kb, S, l, src, dst):
    nc = kb.nc
    T = kb.T
    ntile = T // NT
    NHG = 16
    win = kb.inputs["win%d" % l].rearrange("(c p) f -> p c f", p=128)
    wout = kb.inputs["wout%d" % l].rearrange("(c p) f -> p c f", p=128)
    with ExitStack() as st:
        C = _Common(kb, S, st, l, 4)
        A = C.A
        pk = C.pk
        wi = [A("wi", [128, NCH, 4, 128], BF16) for _ in range(2)]
        oT = A("oT", [128, NCH, NT], BF16)
        Sst = A("Sst", [128, NHG, 128], F32)
        qt = [A("qt", [128, NT], BF16) for _ in range(3)]
        kt = [A("kt", [128, NT], BF16) for _ in range(3)]
        kh = [A("kh", [128, NT], BF16) for _ in range(3)]
        khT = [A("khT", [128, 4, 128], BF16) for _ in range(3)]
        vb = [A("vb", [128, 4, 128], BF16) for _ in range(3)]
        sgate = [A("sgate", [128, NT], F32) for _ in range(3)]
        esc = [A("esc", [128, 8], F32) for _ in range(3)]
        qs2 = [A("qs", [128, NT], F32) for _ in range(2)]
        ff2 = [A("ff", [128, NT], F32)] * 2
        lf2 = [A("lf", [128, NT], F32)] * 2
        kk2 = [A("kk", [128, NT], F32) for _ in range(2)]
        bb2 = [A("bb", [128, NT], F32) for _ in range(2)]
        dd2 = [A("dd", [128, NT], F32) for _ in range(2)]
        ee2 = [A("ee", [128, NT], F32)] * 2
        Sp = [A("Sp", [128, 128], BF16) for _ in range(8)]
        ATm = [A("ATm", [128, 128], BF16) for _ in range(8)]
        osb = A("osb", [128, NT], F32)
        sq = C.junk[:].bitcast(F32)[:, 0:NT]
        rr_ = C.junk[:].bitcast(F32)[:, NT:2 * NT]
        smask = A("smask", [128, NT], F32)
        tril = A("tril", [128, 128], BF16)
        onesf = A("onesf", [128, 128], F32)
        lbe = A("lbe", [128, NCH, 5], F32)
        lbs = A("lbs", [128, NCH], F32)
        lbv = A("lbv", [128, NCH], F32)
        oml = A("oml", [128, NCH], F32)
        epsc = C.epsc
        ng = pk[:, 464:465]
        S.op("sp", lambda e: e.dma_start(out=tril[:], in_=kb.inputs["swamask"][:, 0:128]), writes=["tril"], dma="tril")
        S.op("dve", lambda e: e.memset(onesf[:], 1.0), writes=["onesf"])
        for i8 in range(8):
            S.op("dve", lambda e, i8=i8: e.memset(ATm[i8][:], 0.0), writes=["ATz"])
        S.op("dve", lambda e: e.memset(Sst[:], 0.0), writes=[("S", h) for h in range(NHG)])
        S.op("dve", lambda e: e.memset(smask[:], 1.0), writes=["smask"])
        S.op("dve", lambda e: e.memset(smask[:].rearrange("p (c t) -> p c t", t=128)[:, :, 0:1], 0.0),
             reads=["smask"], writes=["smask"])
        S.op("act", lambda e: e.activation(out=lbe[:], in_=pk[:, 384:464].rearrange("p (c i) -> p c i", i=5), func=AF.Exp),
             reads=["pk"], writes=["lbe"])
        S.op("dve", lambda e: e.tensor_reduce(out=lbs[:], in_=lbe[:], axis=AX.X, op=ALU.add), reads=["lbe"], writes=["lbs"])
        S.op("dve", lambda e: e.reciprocal(out=lbs[:], in_=lbs[:]), reads=["lbs"], writes=["lbs"])
        S.op("dve", lambda e: e.tensor_reduce(out=lbv[:], in_=lbe[:, :, 0:l + 1], axis=AX.X, op=ALU.add),
             reads=["lbe"], writes=["lbv"])
        S.op("dve", lambda e: e.tensor_tensor(out=lbv[:], in0=lbv[:], in1=lbs[:], op=ALU.mult),
             reads=["lbv", "lbs"], writes=["lbv"])
        S.op("dve", lambda e: e.tensor_scalar(out=oml[:], in0=lbv[:], scalar1=-1.0, scalar2=1.0, op0=ALU.mult, op1=ALU.add),
             reads=["lbv"], writes=["oml"])
        kb.wcnt = {"wi": 0, "wo": 0}
        pq, pf, pg, pv = C.bank(0), C.bank(1), C.bank(2), C.bank(3)
        pAT = C.bank(4).rearrange("p (j t) -> p j t", j=4)
        po = C.bank(5)
        pU = C.bank(6).rearrange("p (j t) -> p j t", j=4)
        pss = C.bank(7)
        ptr = C.bank(7).bitcast(BF16)

        def pmm(tt, h, hn_keys):
            slot = kb.wcnt["wi"] % 2
            kb.wcnt["wi"] += 1
            wt = wi[slot]
            for q4 in range(4):
                c0 = q4 * D + h * 128
                S.op("pool", lambda e, wt=wt, q4=q4, c0=c0: e.dma_start(out=wt[:, :, q4, :], in_=win[:, :, c0:c0 + 128],
                                                                        max_dma_last_dim=8192),
                     writes=[("wi", slot, q4)], dma="wi%d_%d" % (slot, q4))
            for q4, pb, bk in ((0, pq, 0), (1, pf, 1), (3, pg, 2)):
                def mm(e, wt=wt, q4=q4, pb=pb):
                    for k in range(NCH):
                        ins = e.matmul(pb, wt[:, k, q4, :], C.hnT[:, k, :], start=(k == 0), stop=(k == NCH - 1))
                    return ins
                S.op("pe", mm, reads=[("wi", slot, q4)] + hn_keys, writes=[("bank", bk)])

            def mmv(e, wt=wt):
                for s in range(4):
                    for k in range(NCH):
                        ins = e.matmul(pv[:, s * 128:(s + 1) * 128], C.hnT[:, k, s * 128:(s + 1) * 128], wt[:, k, 2, :],
                                       start=(k == 0), stop=(k == NCH - 1))
                return ins
            S.op("pe", mmv, reads=[("wi", slot, 2)] + hn_keys, writes=[("bank", 3)])

        def e_ops(h):
            par = h % 3
            tb = h % 2
            qs, kk, bb, dd = qs2[tb], kk2[tb], bb2[tb], dd2[tb]
            ff, lf, ee = ff2[0], lf2[0], ee2[0]
            kq, kkk, kb_, kd = ("qs", tb), ("kk", tb), ("bb", tb), ("dd", tb)
            kf, kl, ke = "ff", "lf", "ee"
            b3 = bb[:].rearrange("p (c t) -> p c t", t=128)
            d3v = dd[:].rearrange("p (c t) -> p c t", t=128)
            O = lambda *a, **k: (a, k)
            G1 = [O("act", lambda e: e.copy(out=vb[par][:], in_=pv.rearrange("p (s d) -> p s d", s=4)),
                    reads=[("bank", 3)], writes=[("vb", par)]),
                  O("act", lambda e: e.activation(out=qs[:], in_=pq, func=AF.Silu), reads=[("bank", 0)], writes=[kq]),
                  O("act", lambda e: e.activation(out=sgate[par][:], in_=pg, func=AF.Silu), reads=[("bank", 2)],
                    writes=[("sgate", par)]),
                  O("act", lambda e: e.activation(out=ff[:], in_=pf, func=AF.Sigmoid), reads=[("bank", 1)], writes=[kf])]
            G2 = [O("dve", lambda e: e.tensor_scalar(out=ff[:], in0=ff[:], scalar1=oml[:, h:h + 1], scalar2=lbv[:, h:h + 1],
                                                     op0=ALU.mult, op1=ALU.add), reads=[kf, "oml", "lbv"], writes=[kf]),
                  O("dve", lambda e: e.tensor_scalar(out=kk[:], in0=ff[:], scalar1=-1.0, scalar2=1.0, op0=ALU.mult,
                                                     op1=ALU.add), reads=[kf], writes=[kkk])]
            G3 = [O("act", lambda e: e.activation(out=lf[:], in_=ff[:], func=AF.Ln), reads=[kf], writes=[kl])]
            G4 = [O("dve", lambda e: e.tensor_tensor_scan(out=bb[:], data0=smask[:], data1=lf[:], initial=0.0,
                                                          op0=ALU.mult, op1=ALU.add), reads=["smask", kl], writes=[kb_]),
                  O("dve", lambda e: e.tensor_tensor(out=d3v, in0=b3, in1=b3[:, :, 63:64].broadcast_to([128, 4, 128]),
                                                     op=ALU.subtract), reads=[kb_], writes=[kd])]
            H1 = [O("act", lambda e: e.activation(out=ee[:], in_=dd[:], func=AF.Exp), reads=[kd], writes=[ke])]
            H2 = [O("dve", lambda e: e.tensor_tensor(out=qt[par][:], in0=qs[:], in1=ee[:], op=ALU.mult),
                    reads=[kq, ke], writes=[("qt", par)])]
            H3 = [O("act", lambda e: e.activation(out=ee[:], in_=dd[:], func=AF.Exp, scale=-1.0), reads=[kd], writes=[ke])]
            H4 = [O("dve", lambda e: e.tensor_tensor(out=kt[par][:], in0=kk[:], in1=ee[:], op=ALU.mult),
                    reads=[kkk, ke], writes=[("kt", par)]),
                  O("dve", lambda e: e.tensor_tensor(out=d3v, in0=b3, in1=b3[:, :, 127:128].broadcast_to([128, 4, 128]),
                                                     op=ALU.subtract), reads=[kb_], writes=[kd])]
            H5 = [O("act", lambda e: e.activation(out=ee[:], in_=dd[:], func=AF.Exp, scale=-1.0), reads=[kd], writes=[ke]),
                  O("act", lambda e: e.activation(out=esc[par][:, 0:4], in_=b3[:, :, 127], func=AF.Exp), reads=[kb_],
                    writes=[("esc", par, 0)]),
                  O("act", lambda e: e.activation(out=esc[par][:, 4:8], in_=b3[:, :, 63], func=AF.Exp), reads=[kb_],
                    writes=[("esc", par, 1)])]
            H6 = [O("dve", lambda e: e.tensor_tensor(out=kh[par][:], in0=kk[:], in1=ee[:], op=ALU.mult),
                    reads=[kkk, ke], writes=[("kh", par)])]
            return [G1, G2, G3, G4], [H1, H2, H3, H4, H5, H6]

        def chainA(tt, h):
            par = h % 3
            hp = h % 2

            def trk(e):
                for j in range(4):
                    ins = e.transpose(ptr[:, j * 128:(j + 1) * 128], kh[par][:, j * 128:(j + 1) * 128], C.ident[:])
                return ins
            S.op("pe", trk, reads=[("kh", par), "ident"], writes=[("bank", 7)])
            S.op("dve", lambda e: e.tensor_copy(out=khT[par][:], in_=ptr[:, 0:512].rearrange("p (j t) -> p j t", j=4)),
                 reads=[("bank", 7)], writes=[("khT", par)])

            def mat(e):
                for j in range(4):
                    c0 = j * 128
                    e.matmul(pAT[0:64, j, 0:64], kt[par][:, c0:c0 + 64], qt[par][:, c0:c0 + 64], start=True, stop=True)
                    ins = e.matmul(pAT[:, j, 64:128], kt[par][:, c0:c0 + 128], qt[par][:, c0 + 64:c0 + 128],
                                   start=True, stop=True)
                return ins
            S.op("pe", mat, reads=[("kt", par), ("qt", par)], writes=[("bank", 4)])

            def mu(e):
                for j in range(4):
                    ins = e.matmul(pU[:, j, :], khT[par][:, j, :], vb[par][:, j, :], start=True, stop=True)
                return ins
            S.op("pe", mu, reads=[("khT", par), ("vb", par)], writes=[("bank", 6)])
            for j in range(4):
                at = ATm[hp * 4 + j]
                S.op("dve", lambda e, at=at, j=j: e.tensor_tensor(out=at[0:64, 0:64], in0=pAT[0:64, j, 0:64],
                                                                  in1=tril[0:64, 0:64], op=ALU.mult),
                     reads=[("bank", 4), "tril", "ATz"], writes=[("ATm", hp, j)])
                S.op("dve", lambda e, at=at, j=j: e.tensor_tensor(out=at[:, 64:128], in0=pAT[:, j, 64:128],
                                                                  in1=tril[:, 64:128], op=ALU.mult),
                     reads=[("bank", 4), "tril", "ATz"], writes=[("ATm", hp, j)])
            for j in range(4):
                sp = Sp[hp * 4 + j]
                S.op("dve", lambda e, sp=sp, j=j: e.tensor_scalar(out=sp[:], in0=Sst[:, h, :], scalar1=esc[par][:, 4 + j:5 + j],
                                                                  scalar2=None, op0=ALU.mult),
                     reads=[("S", h), ("esc", par, 1)], writes=[("Sp", hp, j)])
                S.op("dve", lambda e, j=j: e.scalar_tensor_tensor(out=Sst[:, h, :], in0=Sst[:, h, :],
                                                                  scalar=esc[par][:, j:j + 1], in1=pU[:, j, :],
                                                                  op0=ALU.mult, op1=ALU.add),
                     reads=[("S", h), ("esc", par, 0), ("bank", 6)], writes=[("S", h)])

        def chainC(tt, h):
            par = h % 3
            hp = h % 2

            def mo(e):
                for j in range(4):
                    cs = slice(j * 128, (j + 1) * 128)
                    e.matmul(po[:, cs], Sp[hp * 4 + j][:], qt[par][:, cs], start=True, stop=False)
                    ins = e.matmul(po[:, cs], vb[par][:, j, :], ATm[hp * 4 + j][:], start=False, stop=True)
                return ins
            S.op("pe", mo, reads=[("Sp", hp, j) for j in range(4)] + [("ATm", hp, j) for j in range(4)] +
                 [("qt", par), ("vb", par)], writes=[("bank", 5)])
            S.op("act", lambda e: e.copy(out=osb[:], in_=po), reads=[("bank", 5)], writes=["osb"])
            S.op("act", lambda e: e.activation(out=sq, in_=osb[:], func=AF.Square), reads=["osb"], writes=["junk"])
            S.op("pe", lambda e: e.matmul(pss, onesf[:], sq, start=True, stop=True), reads=["onesf", "junk"],
                 writes=[("bank", 7)])
            S.op("act", lambda e: e.activation(out=rr_, in_=pss, func=AF.Ln, scale=1.0 / 128, bias=epsc[:, 0:1]),
                 reads=[("bank", 7), "epsc"], writes=["rr"])
            S.op("act", lambda e: e.activation(out=rr_, in_=rr_, func=AF.Exp, scale=-0.5), reads=["rr"], writes=["rr"])
            S.op("dve", lambda e: e.tensor_tensor(out=osb[:], in0=osb[:], in1=rr_, op=ALU.mult),
                 reads=["osb", "rr"], writes=["osb"])
            S.op("dve", lambda e: e.scalar_tensor_tensor(out=oT[:, h, :], in0=osb[:], scalar=ng, in1=sgate[par][:],
                                                         op0=ALU.mult, op1=ALU.mult),
                 reads=["osb", "pk", ("sgate", par)], writes=[("oT", h)])

        for tt in range(ntile):
            hn_keys = C.pre_norm(kb, S, src, tt)
            eops = {}
            for i in range(NHG + 3):
                if i < NHG:
                    pmm(tt, i, hn_keys)
                    eops[i] = e_ops(i)
                if 0 <= i - 3 < NHG:
                    chainC(tt, i - 3)
                if 0 <= i - 2 < NHG:
                    chainA(tt, i - 2)
                g1 = eops[i][0] if i < NHG else []
                g2 = eops[i - 1][1] if 0 <= i - 1 < NHG else []
                for k in range(max(len(g1), len(g2))):
                    for grp in ((g1[k] if k < len(g1) else []), (g2[k] if k < len(g2) else [])):
                        for (a_, k_) in grp:
                            S.op(*a_, **k_)
            C.out_proj(kb, S, src, dst, tt, oT, wout, 4, lambda pc, s: [("oT", pc * 4 + k) for k in range(4)], pipe_src=src)
        S.emit()


def mixer_phase(kb, S, l, m, src, dst):
    if m == 2:
        sconv_phase(kb, S, l, src, dst)
    elif m == 1:
        swa_phase(kb, S, l, src, dst)
    elif m == 3:
        fox_phase(kb, S, l, src, dst)
    elif m == 0:
        hgrn_phase(kb, S, l, src, dst)
    else:
        raise NotImplementedError(m)


def swa_mask():
    k = np.arange(128)[:, None]
    q = np.arange(128)[None, :]
    return np.concatenate([(k <= q), (k > q)], axis=1).astype(ml_dtypes.bfloat16)


def mixer_inputs(l, inp, m, j, T):
    out = {}
    if m == 2:
        out["win%d" % l] = np.asarray(inp["sc_w_in"][j])
        out["wout%d" % l] = np.asarray(inp["sc_w_out"][j])
    elif m == 1:
        w = np.asarray(inp["swa_w_in"][j])
        wq, wk, wv = w[:, :2048], w[:, 2048:2304], w[:, 2304:2560]
        wk2 = np.concatenate([np.concatenate([wk[:, g * 64:(g + 1) * 64]] * 2, axis=1) for g in range(4)], axis=1)
        out["win%d" % l] = np.ascontiguousarray(np.concatenate([wq, wk2, wv], axis=1))
        out["wout%d" % l] = np.asarray(inp["swa_w_out"][j])
        out["posT"] = np.ascontiguousarray(np.asarray(inp["positions"]).astype(np.int32).reshape(-1, 128).T)
        out["invf"] = np.tile((np.float32(500000.0) ** (-(np.arange(8, dtype=np.float32)) / 8)).astype(np.float32), (128, 1))
        out["swamask"] = swa_mask()
    elif m == 0:
        out["win%d" % l] = np.asarray(inp["hgrn_w_in"][j])
        out["wout%d" % l] = np.asarray(inp["hgrn_w_out"][j])
        out["swamask"] = swa_mask()
    elif m == 3:
        out["win%d" % l] = np.asarray(inp["fox_w_in"][j])
        out["wout%d" % l] = np.asarray(inp["fox_w_out"][j])
        out["swamask"] = swa_mask()
        out["utri"] = np.triu(np.ones((128, 128), np.float32))
        out["identf"] = np.eye(128, dtype=np.float32)
    return out


_CACHE = {}


def make_in_maps(inputs, T, layers, ncores):
    inp = {k: np.asarray(v) for k, v in inputs.items()}
    shared = {"ident": np.eye(128).astype(ml_dtypes.bfloat16)}
    for l, m in enumerate(layers):
        j = l // 4
        pk, rows = pack_layer(l, inp, m, j)
        shared["pk%d" % l] = pk
        shared["rows%d" % l] = rows
        shared["wup%d" % l] = inp["ffn_w_up"][l]
        shared["wdn%d" % l] = inp["ffn_w_down"][l]
        shared.update(mixer_inputs(l, inp, m, j, T))
    x = inp["x"]
    maps = []
    for c in range(ncores):
        d = dict(shared)
        d["x"] = np.ascontiguousarray(x[c])
        maps.append(d)
    return maps


def kernel(**inputs):
    x = np.asarray(inputs["x"])
    B, T, _ = x.shape
    layers = [i % 4 for i in range(DEPTH)]
    key = (T, tuple(layers))
    if key not in _CACHE:
        _CACHE[key] = build(T, layers)
    kb = _CACHE[key]
    maps = make_in_maps(inputs, T, layers, B)
    maps = [{k: v for k, v in m.items() if k in kb.inputs} for m in maps]
    res = run_bass_kernel_spmd(kb.nc, maps, core_ids=list(range(B)))
    return np.stack([np.asarray(r["y"]) for r in res.results], axis=0).astype(np.float32)
```
